# Optimizing a Trainium2 kernel written in Bass

```python
import jax, jax.numpy as jnp
from jax import lax
import numpy as np

D_MODEL = 1024
BATCH = 16
SEQ = 4096
DEPTH = 4

CHUNK = 64
N_MIXERS = 4
EPS = 1e-6
A_BLOCK = 128
A_HEADS = 8
A_WIDTH = D_MODEL
A_HEAD_DIM = A_WIDTH // A_HEADS
B_WINDOWS = (2, 4, 8, 16)
B_GROUPS = 4
B_WIDTH = D_MODEL
B_GROUP_DIM = B_WIDTH // B_GROUPS
C_WIDTH = D_MODEL
C_CONV = 3
D_WIDTH = D_MODEL
D_CONV = 31
D_FF = 2816
FFN_CONV = 3

kernel_name = 'chunk_causal_hybrid_conv_mlp_trunk'


def _n_layers_of(m):
    return (DEPTH - m + N_MIXERS - 1) // N_MIXERS


def _rmsnorm(x, g):
    xf = x.astype(jnp.float32)
    y = xf * lax.rsqrt(jnp.mean(xf * xf, axis=-1, keepdims=True) + EPS)
    return (y * g.astype(jnp.float32)).astype(x.dtype)


def _layernorm(x, g, b):
    xf = x.astype(jnp.float32)
    mu = jnp.mean(xf, axis=-1, keepdims=True)
    xc = xf - mu
    var = jnp.mean(xc * xc, axis=-1, keepdims=True)
    y = xc * lax.rsqrt(var + EPS) * g.astype(jnp.float32) + b.astype(jnp.float32)
    return y.astype(x.dtype)


def _causal_dwconv(x, w):
    k = w.shape[0]
    return lax.conv_general_dilated(
        x, w[:, None, :].astype(x.dtype), window_strides=(1,), padding=[(k - 1, 0)],
        dimension_numbers=('NWC', 'WIO', 'NWC'), feature_group_count=x.shape[-1])


def _mixer_a(h, w_in, w_s, b_s, ln_g, ln_b, w_out):
    bsz, s, _ = h.shape
    uv = jax.nn.gelu(h @ w_in, approximate=False)
    u, v = jnp.split(uv, 2, axis=-1)
    v = _layernorm(v, ln_g, ln_b)
    cpos = jnp.arange(A_BLOCK) // CHUNK
    mask = (cpos[None, :] <= cpos[:, None]).astype(w_s.dtype)
    w_m = w_s * mask[None]
    v = v.reshape(bsz, s // A_BLOCK, A_BLOCK, A_HEADS, A_HEAD_DIM)
    z = jnp.einsum('gqp,bnpgc->bnqgc', w_m, v) + b_s.T[None, None, :, :, None]
    z = z.reshape(bsz, s, A_WIDTH)
    return (u * z) @ w_out


def _mixer_b(h, w_in, w_grp, scale, w_out):
    bsz, s, _ = h.shape
    p = (h @ w_in).astype(jnp.float32).reshape(bsz, s, B_GROUPS, B_GROUP_DIM)
    cs = jnp.concatenate([jnp.zeros_like(p[:, :1]), jnp.cumsum(p, axis=1)], axis=1)
    t1 = jnp.arange(1, s + 1)
    outs = []
    for g, w in enumerate(B_WINDOWS):
        lo = jnp.maximum(t1 - w, 0)
        cnt = jnp.minimum(t1, w).astype(jnp.float32)
        win = cs[:, 1:, g] - jnp.take(cs[:, :, g], lo, axis=1)
        outs.append(win / cnt[None, :, None] - p[:, :, g])
    pooled = jnp.stack(outs, axis=2).astype(h.dtype)
    mixed = jnp.einsum('bsgc,gcd->bsgd', pooled, w_grp).reshape(bsz, s, B_WIDTH)
    return (mixed * scale) @ w_out


def _mixer_c(h, w_in, conv_w, w_out):
    bg, cg, xv = jnp.split(h @ w_in, 3, axis=-1)
    return (bg * _causal_dwconv(cg * xv, conv_w)) @ w_out


def _mixer_d(h, w1, b1, conv_w, conv_b, ln_g, ln_b, w2, b2):
    a, gate = jnp.split(h @ w1 + b1, 2, axis=-1)
    z = a * jax.nn.sigmoid(gate)
    z = _causal_dwconv(z, conv_w) + conv_b
    z = jax.nn.silu(_layernorm(z, ln_g, ln_b))
    return z @ w2 + b2


def _channel_mixer(h, w_up, conv_w, w_down):
    g, u = jnp.split(_causal_dwconv(h @ w_up, conv_w), 2, axis=-1)
    return (jax.nn.silu(g) * u) @ w_down


def setup_inputs(seed: int = 0) -> dict:
    key = jax.random.key(seed)
    ks = iter(jax.random.split(key, 40))

    def nrm(shape, scale):
        return jax.random.normal(next(ks), shape, jnp.float32) * scale

    def gain(shape):
        return 1.0 + nrm(shape, 0.05)

    na, nb, nc, nd = (_n_layers_of(m) for m in range(N_MIXERS))
    d = D_MODEL
    return {
        'x': nrm((BATCH, SEQ, d), 1.0),
        'norm_mix_pre': gain((DEPTH, d)),
        'norm_mix_post': gain((DEPTH, d)),
        'norm_ffn_pre': gain((DEPTH, d)),
        'norm_ffn_post': gain((DEPTH, d)),
        'a_w_in': nrm((na, d, 2 * A_WIDTH), d ** -0.5),
        'a_w_s': nrm((na, A_HEADS, A_BLOCK, A_BLOCK), A_BLOCK ** -0.5),
        'a_b_s': gain((na, A_HEADS, A_BLOCK)),
        'a_ln_g': gain((na, A_WIDTH)),
        'a_ln_b': nrm((na, A_WIDTH), 0.02),
        'a_w_out': nrm((na, A_WIDTH, d), A_WIDTH ** -0.5),
        'b_w_in': nrm((nb, d, B_WIDTH), d ** -0.5),
        'b_w_grp': nrm((nb, B_GROUPS, B_GROUP_DIM, B_GROUP_DIM), B_GROUP_DIM ** -0.5),
        'b_scale': gain((nb, B_WIDTH)),
        'b_w_out': nrm((nb, B_WIDTH, d), B_WIDTH ** -0.5),
        'c_w_in': nrm((nc, d, 3 * C_WIDTH), d ** -0.5),
        'c_conv_w': nrm((nc, C_CONV, C_WIDTH), C_CONV ** -0.5),
        'c_w_out': nrm((nc, C_WIDTH, d), C_WIDTH ** -0.5),
        'd_w1': nrm((nd, d, 2 * D_WIDTH), d ** -0.5),
        'd_b1': nrm((nd, 2 * D_WIDTH), 0.02),
        'd_conv_w': nrm((nd, D_CONV, D_WIDTH), D_CONV ** -0.5),
        'd_conv_b': nrm((nd, D_WIDTH), 0.02),
        'd_ln_g': gain((nd, D_WIDTH)),
        'd_ln_b': nrm((nd, D_WIDTH), 0.02),
        'd_w2': nrm((nd, D_WIDTH, d), D_WIDTH ** -0.5),
        'd_b2': nrm((nd, d), 0.02),
        'f_w_up': nrm((DEPTH, d, 2 * D_FF), d ** -0.5),
        'f_conv_w': nrm((DEPTH, FFN_CONV, 2 * D_FF), FFN_CONV ** -0.5),
        'f_w_down': nrm((DEPTH, D_FF, d), D_FF ** -0.5),
    }


def reference(x, norm_mix_pre, norm_mix_post, norm_ffn_pre, norm_ffn_post,
              a_w_in, a_w_s, a_b_s, a_ln_g, a_ln_b, a_w_out,
              b_w_in, b_w_grp, b_scale, b_w_out,
              c_w_in, c_conv_w, c_w_out,
              d_w1, d_b1, d_conv_w, d_conv_b, d_ln_g, d_ln_b, d_w2, d_b2,
              f_w_up, f_conv_w, f_w_down):
    for i in range(DEPTH):
        m, j = i % N_MIXERS, i // N_MIXERS
        h = _rmsnorm(x, norm_mix_pre[i])
        if m == 0:
            y = _mixer_a(h, a_w_in[j], a_w_s[j], a_b_s[j], a_ln_g[j], a_ln_b[j], a_w_out[j])
        elif m == 1:
            y = _mixer_b(h, b_w_in[j], b_w_grp[j], b_scale[j], b_w_out[j])
        elif m == 2:
            y = _mixer_c(h, c_w_in[j], c_conv_w[j], c_w_out[j])
        else:
            y = _mixer_d(h, d_w1[j], d_b1[j], d_conv_w[j], d_conv_b[j],
                         d_ln_g[j], d_ln_b[j], d_w2[j], d_b2[j])
        x = x + _rmsnorm(y, norm_mix_post[i])
        h = _rmsnorm(x, norm_ffn_pre[i])
        y = _channel_mixer(h, f_w_up[i], f_conv_w[i], f_w_down[i])
        x = x + _rmsnorm(y, norm_ffn_post[i])
    return x
```

```python
import numpy as np
import concourse.bass as bass
import concourse.mybir as mybir
from concourse.bass_utils import run_bass_kernel_spmd

F32 = mybir.dt.float32
BF16 = mybir.dt.bfloat16
AF = mybir.ActivationFunctionType
ALU = mybir.AluOpType

D = 1024
NCH = 8
DFF = 2816
NPAIR = 22
SEQ = 4096
DEPTH = 4
EPS = 1e-6
RING = 4
G_DVE_MOD = 2
G_DVE_CNT = 1
CHAIN_SLACK = 3
SLOT = 4096

ENGS = ("pe", "act", "dve", "pool", "sp")


class Sched:
    def __init__(self, nc):
        self.nc = nc
        self.streams = {e: [] for e in ENGS}
        self.prog = {e: nc.alloc_semaphore(name=f"prog_{e}") for e in ENGS}
        self.count = {e: 0 for e in ENGS}
        self.seen = {e: {} for e in ENGS}
        self.sems = {("prog", e): self.prog[e] for e in ENGS}
        self.dma_count = {}
        self.reg = {}
        self.tag = ""

    def dma_sem(self, name):
        k = ("dma", name)
        if k not in self.sems:
            self.sems[k] = self.nc.alloc_semaphore(name=f"dma_{name}")
            self.dma_count[name] = 0
        return k

    def _need(self, eng, tok, waits):
        if tok is None:
            return
        k, v = tok
        if eng == "pe" and k == ("prog", "pe"):
            return
        if self.seen[eng].get(k, 0) >= v:
            return
        waits[k] = max(waits.get(k, 0), v)

    def op(self, eng, fn, reads=(), writes=(), dma=None, inc=True):
        waits = {}
        for r in reads:
            w, rd = self.reg.get(r, (None, {}))
            self._need(eng, w, waits)
            if r.startswith("PS") and eng in ("act", "dve"):
                other = ("prog", "dve" if eng == "act" else "act")
                if other in rd:
                    self._need(eng, (other, rd[other]), waits)
        for r in writes:
            w, rd = self.reg.get(r, (None, {}))
            self._need(eng, w, waits)
            for k, v in rd.items():
                if k == ("prog", eng):
                    continue
                self._need(eng, (k, v), waits)
        for k, v in waits.items():
            self.seen[eng][k] = v
        if dma is None and not inc:
            assert eng == "pe"
            tok = (("prog", eng), self.count[eng] + 1)
            inc = None
        elif dma is None:
            self.count[eng] += 1
            tok = (("prog", eng), self.count[eng])
            inc = (self.prog[eng], 1)
        else:
            k = self.dma_sem(dma)
            self.dma_count[dma] += 16
            tok = (k, self.dma_count[dma])
            inc = (self.sems[k], 16)
        self.streams[eng].append((list(waits.items()), fn, inc, self.tag))
        for r in reads:
            w, rd = self.reg.get(r, (None, {}))
            rd = dict(rd)
            rd[tok[0]] = max(rd.get(tok[0], 0), tok[1])
            self.reg[r] = (w, rd)
        for r in writes:
            self.reg[r] = (tok, {})
        return tok

    def final_wait(self, eng, toks):
        waits = {}
        for t in toks:
            self._need(eng, t, waits)
        self.streams[eng].append((list(waits.items()), None, None, "final"))

    def replay(self, eng, e):
        for waits, fn, inc, _tag in self.streams[eng]:
            for k, v in waits:
                e.wait_ge(self.sems[k], v)
            if fn is not None:
                ins = fn(e)
                if inc is not None:
                    ins.then_inc(inc[0], inc[1])

    def emit(self):
        with self.nc.Block() as block:
            @block.tensor
            def _(e):
                self.replay("pe", e)

            @block.scalar
            def _(e):
                self.replay("act", e)

            @block.vector
            def _(e):
                self.replay("dve", e)

            @block.gpsimd
            def _(e):
                self.replay("pool", e)

            @block.sync
            def _(e):
                self.replay("sp", e)


def lin_block(W, col0, ncols=128):
    K = W.shape[0]
    return W[:, col0:col0 + ncols].reshape(K // 128, 128, ncols).transpose(1, 0, 2)


def diag_blocks(vecs):
    out = np.zeros((128, len(vecs) * 128), np.float32)
    idx = np.arange(128)
    for i, v in enumerate(vecs):
        out[idx, i * 128 + idx] = v
    return out


def layer_pieces(l, inp):
    m = l % 4
    out = []

    def add(tag, F, fn):
        out.append((tag, F, fn))

    def lin_piece(W_fn, cols, nk, extra_F=0, extra_fn=None):
        F = len(cols) * nk * 128 + extra_F
        def fn():
            W = W_fn()
            parts = [lin_block(W, c).reshape(128, nk * 128) for c in cols]
            if extra_fn is not None:
                parts.append(extra_fn())
            return np.concatenate(parts, axis=1)
        return F, fn

    if m == 0:
        Wi = lambda: inp["a_w_in"][0]
        for h in range(2):
            add(f"a_v{h}", 8 * 512, (lambda h=h: lin_block(Wi(), 1024 + h * 512, 512).reshape(128, 8 * 512)))
        for h in range(2):
            F, fn = lin_piece(Wi, [(h * 4 + jj) * 128 for jj in range(4)], 8)
            add(f"a_u{h}", F, fn)
        Wo = lambda: inp["a_w_out"][0]
    elif m == 1:
        Wi = lambda: inp["b_w_in"][0]
        for h in range(2):
            F, fn = lin_piece(Wi, [(h * 4 + jj) * 128 for jj in range(4)], 8)
            add(f"b_in{h}", F, fn)

        def grp():
            Wg = inp["b_w_grp"][0]
            blocks = []
            for g in range(4):
                for dj in range(2):
                    blocks.append(lin_block(Wg[g], dj * 128).reshape(128, 2 * 128))
            return np.concatenate(blocks, axis=1)
        add("b_grp", 4 * 2 * 2 * 128, grp)
        Wo = lambda: inp["b_w_out"][0]
    elif m == 2:
        Wi = lambda: inp["c_w_in"][0]
        for j in range(8):
            ex = lambda j=j: diag_blocks([inp["c_conv_w"][0][k, j * 128:(j + 1) * 128] for k in range(3)])
            F, fn = lin_piece(Wi, [j * 128, 1024 + j * 128, 2048 + j * 128], 8, 3 * 128, ex)
            add(f"c_in{j}", F, fn)
        Wo = lambda: inp["c_w_out"][0]
    else:
        Wi = lambda: inp["d_w1"][0]
        for q in range(4):
            cols = []
            for pp in range(2):
                j = q * 2 + pp
                cols += [j * 128, 1024 + j * 128]
            F, fn = lin_piece(Wi, cols, 8)
            add(f"d_in{q}", F, fn)
        for j in range(8):
            add(f"d_cv{j}", 31 * 128, (lambda j=j: diag_blocks([inp["d_conv_w"][0][k, j * 128:(j + 1) * 128] for k in range(31)])))
        Wo = lambda: inp["d_w2"][0]
    for h in range(2):
        F, fn = lin_piece(Wo, [(h * 4 + jj) * 128 for jj in range(4)], 8)
        add(f"m_out{h}", F, fn)
    Wu = lambda: inp["f_w_up"][l]

    def gdiag(p):
        cw = inp["f_conv_w"][l]
        if p < 0:
            return np.zeros((128, 3 * 128), np.float32)
        return diag_blocks([cw[k, p * 128:(p + 1) * 128] for k in range(3)])
    for p in range(NPAIR):
        F, fn = lin_piece(Wu, [p * 128, DFF + p * 128], 8, 3 * 128, (lambda p=p: gdiag(p - 1)))
        add(f"f_up{p}", F, fn)
    Wd = lambda: inp["f_w_down"][l]
    for j in range(8):
        if j == 0:
            F, fn = lin_piece(Wd, [j * 128], NPAIR, 3 * 128, (lambda: gdiag(NPAIR - 1)))
        else:
            F, fn = lin_piece(Wd, [j * 128], NPAIR)
        add(f"f_dn{j}", F, fn)
    return out


def piece_table():
    tab = []
    off = 0
    for l in range(DEPTH):
        for tag, F, _ in layer_pieces(l, None):
            tab.append((l, tag, off, F))
            off += F
    return tab, off


def cpp_layout():
    off = {}
    n = 0
    def add(k, w):
        nonlocal n
        off[k] = n
        n += w
    add("ng", 16 * 8)
    add("fcw", 4 * 44 * 3)
    add("bsc", 8)
    add("ccw", 8 * 3)
    add("db1", 16)
    add("dcw", 8 * 31)
    add("dcb", 8)
    add("dlg", 8)
    add("dlb", 8)
    add("db2", 8)
    return off, n


def cbc_layout():
    off = {}
    n = 0
    def add(k, w):
        nonlocal n
        off[k] = n
        n += w
    add("bs", 8 * 128)
    add("lng", 1024)
    add("lnb", 1024)
    add("invc", 4 * 16)
    return off, n


def colvec(v):
    return np.ascontiguousarray(v.reshape(-1, 128).T)


def pack_host(inp):
    tab, WF = piece_table()
    wflat = np.empty((128, WF), np.float32)
    i = 0
    for l in range(DEPTH):
        for tag, F, fn in layer_pieces(l, inp):
            _, _, off, F2 = tab[i]
            wflat[:, off:off + F] = fn()
            i += 1
    co, ncp = cpp_layout()
    cpp = np.zeros((128, ncp), np.float32)
    for t, key in enumerate(["norm_mix_pre", "norm_mix_post", "norm_ffn_pre", "norm_ffn_post"]):
        for l in range(DEPTH):
            o = co["ng"] + (t * 4 + l) * 8
            cpp[:, o:o + 8] = colvec(inp[key][l])
    for l in range(DEPTH):
        cw = inp["f_conv_w"][l]
        for k in range(3):
            cv = colvec(cw[k])
            for c in range(44):
                cpp[:, co["fcw"] + (l * 44 + c) * 3 + k] = cv[:, c]
    cpp[:, co["bsc"]:co["bsc"] + 8] = colvec(inp["b_scale"][0])
    for k in range(3):
        cv = colvec(inp["c_conv_w"][0][k])
        for j in range(8):
            cpp[:, co["ccw"] + j * 3 + k] = cv[:, j]
    cpp[:, co["db1"]:co["db1"] + 16] = colvec(inp["d_b1"][0])
    for k in range(31):
        cv = colvec(inp["d_conv_w"][0][k])
        for j in range(8):
            cpp[:, co["dcw"] + j * 31 + k] = cv[:, j]
    cpp[:, co["dcb"]:co["dcb"] + 8] = colvec(inp["d_conv_b"][0])
    cpp[:, co["dlg"]:co["dlg"] + 8] = colvec(inp["d_ln_g"][0])
    cpp[:, co["dlb"]:co["dlb"] + 8] = colvec(inp["d_ln_b"][0])
    cpp[:, co["db2"]:co["db2"] + 8] = colvec(inp["d_b2"][0])
    bo, nbc = cbc_layout()
    cbc = np.zeros((128, nbc), np.float32)
    cbc[:, bo["bs"]:bo["bs"] + 1024] = inp["a_b_s"][0].reshape(1, 1024)
    cbc[:, bo["lng"]:bo["lng"] + 1024] = inp["a_ln_g"][0].reshape(1, 1024)
    cbc[:, bo["lnb"]:bo["lnb"] + 1024] = inp["a_ln_b"][0].reshape(1, 1024)
    for g, w in enumerate((2, 4, 8, 16)):
        for t in range(16):
            cbc[:, bo["invc"] + g * 16 + t] = 1.0 / min(t + 1, w)
    wst = np.ascontiguousarray(inp["a_w_s"][0].transpose(2, 0, 1)).reshape(128, 1024)
    return wflat, cpp, cbc, wst


class Ctx:
    pass


def build_program(ntok, T=512, layers=(0, 1, 2, 3), seq=SEQ):
    nc = bass.Bass("TRN2", target_bir_lowering=False)
    tab, WF = piece_table()
    co, ncp = cpp_layout()
    bo, nbc = cbc_layout()
    xT = nc.dram_tensor("xT", [128, NCH, ntok], F32, kind="ExternalInput").ap()
    wflat = nc.dram_tensor("wflat", [128, WF], F32, kind="ExternalInput").ap()
    cppd = nc.dram_tensor("cpp", [128, ncp], F32, kind="ExternalInput").ap()
    cbcd = nc.dram_tensor("cbc", [128, nbc], F32, kind="ExternalInput").ap()
    wstd = nc.dram_tensor("wst", [128, 1024], F32, kind="ExternalInput").ap()
    oT = nc.dram_tensor("oT", [128, NCH, ntok], F32, kind="ExternalOutput").ap()
    wbf = nc.dram_tensor("wbf", [128, WF], BF16, kind="Internal").ap()

    S = Sched(nc)
    sb = nc.alloc_sbuf_tensor
    HW = T + 32
    Y = sb("Y", [128, NCH, T], F32)
    ACTB = sb("ACTB", [128, NPAIR, T], BF16)
    M1 = sb("M1", [128, NCH, HW], F32)
    M2 = sb("M2", [128, NCH, HW], F32)
    NTMP = 6
    TMPT = sb("TMP", [128, NTMP, T], F32)
    TMP = [TMPT[:, i, :] for i in range(NTMP)]
    VT = TMPT[:, 4:6, :].rearrange("p a t -> p (a t)")
    VTR = ["TMP4", "TMP5"]
    def dve_conv(gu, p):
        return gu == 1 or (p % G_DVE_MOD) < G_DVE_CNT
    GB = [[sb(f"GB{g}{i}", [128, T + 2], BF16) for i in range(2)] for g in range(2)]
    AS = [[sb(f"AS{g}{i}", [128, T], F32) for i in range(2)] for g in range(2)]
    ZV = sb("ZV", [128, NCH * HW], BF16)
    ZB = ZV[:, :].rearrange("p (c w) -> p c w", w=HW)
    SLOTS = [sb(f"WS{i}", [128, SLOT], BF16) for i in range(RING)]
    CPP = sb("CPP", [128, ncp], F32)
    CBC = sb("CBC", [128, nbc], F32)
    WST = sb("WST", [128, 8, 128], BF16)
    ONES = sb("ONES", [128, 128], BF16)
    ZERO2 = sb("ZERO2", [128, 88], BF16)
    VNB = ZV[:, 0:4096].rearrange("p (b c) -> p b c", c=1024)

    def zb_over(blk):
        return [f"ZB.{j}" for j in range((1024 * blk) // HW, min(NCH - 1, (1024 * blk + 1023) // HW) + 1)]

    def vnb_over(j):
        return [f"VNB.{b}" for b in range((HW * j) // 1024, min(3, (HW * j + HW - 1) // 1024) + 1)]
    ST = sb("ST", [128, 24], F32)
    MV = sb("MV", [128, 8], F32)
    PS = [nc.alloc_psum_tensor(f"PS{i}", [128, 512], F32) for i in range(8)]
    print("sbuf remaining before streams", nc.sbuf_bytes_remaining)

    half = ntok // 2
    assert half % seq == 0 or half == seq
    ctxs = []
    for pi, pfx in enumerate("AB"):
        c = Ctx()
        c.p = pfx
        c.tok0 = pi * half
        c.X = sb(f"X{pfx}", [128, NCH, T], F32)
        c.H = sb(f"H{pfx}", [128, NCH, T], BF16)
        c.CF = [sb(f"CF{pfx}{l}", [128, 44, 2], BF16) for l in range(DEPTH)]
        c.CB = sb(f"CB{pfx}", [128, NCH, 16], F32)
        c.CC = sb(f"CC{pfx}", [128, NCH, 2], BF16)
        c.CD = sb(f"CD{pfx}", [128, NCH, 32], BF16)
        ctxs.append(c)
    CA, CBX = ctxs
    ntiles = half // T

    state = {"bank": 0, "tmp": 0}

    def newbank():
        b = state["bank"]
        state["bank"] = (b + 1) % 8
        return b

    def newtmp(n=3):
        i = state["tmp"] % n
        state["tmp"] = (i + 1) % n
        return i

    def cp(key, idx):
        o = co[key] + idx
        return CPP[:, o:o + 1]

    def xr(c, n=NCH):
        return [f"{c.p}X.{j}" for j in range(n)]

    def hr(c, n=NCH):
        return [f"{c.p}H.{j}" for j in range(n)]

    def allr(name, n=NCH):
        return [f"{name}.{j}" for j in range(n)]

    sub_list = [(l, k) for l in layers for k in (0, 1)]
    def sub_pieces(l, k):
        out = []
        for pi, (pl, tag, off, F) in enumerate(tab):
            if pl == l and (tag.startswith("f_") == (k == 1)):
                out.append((pi, off, F))
        return out
    glob = []
    for ti in range(ntiles):
        for (l, k) in sub_list:
            for si in range(2):
                for (pi, off, F) in sub_pieces(l, k):
                    glob.append((pi, off, F, ti == 0 and si == 0))
    ring = {"issued": 0}

    def ring_need(lo):
        while ring["issued"] < min(len(glob), lo + RING):
            q = ring["issued"]
            pi, off, F, first = glob[q]
            sl = q % RING
            if first:
                S.op("pool", lambda e, sl=sl, off=off, F=F: e.dma_start(out=SLOTS[sl][:, 0:F], in_=wflat[:, off:off + F]),
                     writes=[f"W{sl}"], dma=f"w{sl}")
                S.op("sp", lambda e, sl=sl, off=off, F=F: e.dma_start(out=wbf[:, off:off + F], in_=SLOTS[sl][:, 0:F]),
                     reads=[f"W{sl}"], writes=[f"WBF{pi}"], dma=f"wb{q % 4}")
            else:
                S.op("sp", lambda e, sl=sl, off=off, F=F: e.dma_start(out=SLOTS[sl][:, 0:F], in_=wbf[:, off:off + F]),
                     reads=[f"WBF{pi}"], writes=[f"W{sl}"], dma=f"w{sl}")
            ring["issued"] += 1

    piece_ctr = {"n": 0}

    def acquire(hold=0):
        s = piece_ctr["n"]
        piece_ctr["n"] += 1
        ring_need(s - hold)
        return s % RING

    def MM(b, lhsT, rhs, start, stop, reads, cols=None):
        out = PS[b][:, :] if cols is None else PS[b][:, cols[0]:cols[1]]
        S.op("pe", lambda e: e.matmul(out, lhsT, rhs, start=start, stop=stop), reads=reads, writes=[f"PS{b}"], inc=stop)

    S.op("sp", lambda e: e.dma_start(out=CPP[:, :], in_=cppd[:, :]), writes=["CPP"], dma="c")
    S.op("sp", lambda e: e.dma_start(out=CBC[:, :], in_=cbcd[:, :]), writes=["CBC"], dma="c")
    S.op("sp", lambda e: e.dma_start(out=VT, in_=wstd[:, :]), writes=VTR, dma="c")
    S.op("dve", lambda e: e.memset(ONES[:, :], 1.0 / 1024.0), writes=["ONES"])
    S.op("dve", lambda e: e.memset(ZERO2[:, :], 0.0), writes=["ZERO2"])
    S.op("dve", lambda e: e.tensor_copy(WST[:, :, :], VT.rearrange("p (g q) -> p g q", g=8)), reads=VTR, writes=["WST"])
    S.op("dve", lambda e: e.memset(WST[64:128, :, 0:64], 0.0), reads=["WST"], writes=["WST"])

    def rstd_from_bank(b, dst):
        S.op("act", lambda e: e.activation(out=TMP[dst], in_=PS[b][:, :], func=AF.Sqrt, bias=EPS),
             reads=[f"PS{b}"], writes=[f"TMP{dst}"])
        S.op("dve", lambda e: e.reciprocal(TMP[dst], TMP[dst]), reads=[f"TMP{dst}"], writes=[f"TMP{dst}"])

    def pre_sq(c):
        for j in range(NCH):
            S.tag = "pre"
            S.op("act", lambda e, j=j: e.activation(out=c.H[:, j, :], in_=c.X[:, j, :], func=AF.Square),
                 reads=[f"{c.p}X.{j}"], writes=[f"{c.p}H.{j}"])
            if j % 4 == 3:
                yield

    def pre_norm(c, gidx, per=2):
        S.tag = "pre"
        b = newbank()
        for j in range(NCH):
            MM(b, ONES[:, :], c.H[:, j, :], j == 0, j == NCH - 1, ["ONES", f"{c.p}H.{j}"])
        r = 3
        rstd_from_bank(b, r)
        yield
        for j in range(NCH):
            S.tag = "pre"
            S.op("dve", lambda e, j=j: e.scalar_tensor_tensor(out=c.H[:, j, :], in0=c.X[:, j, :], scalar=cp("ng", gidx * 8 + j),
                                                             in1=TMP[r], op0=ALU.mult, op1=ALU.mult),
                 reads=[f"{c.p}X.{j}", f"TMP{r}", "CPP"], writes=[f"{c.p}H.{j}"])
            if j % per == per - 1:
                yield

    def pre_stage(c, gidx):
        for _ in pre_sq(c):
            pass
        for _ in pre_norm(c, gidx):
            pass

    def post_stage(c, gidx, per=2):
        S.tag = "post"
        b = newbank()
        for j in range(NCH):
            MM(b, ONES[:, :], c.H[:, j, :], j == 0, j == NCH - 1, ["ONES", f"{c.p}H.{j}"])
        r = 3
        rstd_from_bank(b, r)
        yield
        for j in range(NCH):
            S.tag = "post"
            S.op("dve", lambda e, j=j: e.scalar_tensor_tensor(out=Y[:, j, :], in0=Y[:, j, :], scalar=cp("ng", gidx * 8 + j),
                                                             in1=TMP[r], op0=ALU.mult, op1=ALU.mult),
                 reads=[f"Y.{j}", f"TMP{r}", "CPP"], writes=[f"Y.{j}"])
            S.op("dve", lambda e, j=j: e.tensor_tensor(out=c.X[:, j, :], in0=c.X[:, j, :], in1=Y[:, j, :], op=ALU.add),
                 reads=[f"{c.p}X.{j}", f"Y.{j}"], writes=[f"{c.p}X.{j}"])
            if j % per == per - 1:
                yield

    def outproj(c, nk, rhs_of, rhs_reads, bias_key=None, first_slot=None):
        for j in range(NCH):
            S.tag = "outproj"
            if nk == NCH:
                if j % 4 == 0:
                    sl = acquire()
                base = (j % 4) * nk * 128
            else:
                sl = first_slot if (j == 0 and first_slot is not None) else acquire()
                base = 0
            b = newbank()
            for k in range(nk):
                MM(b, SLOTS[sl][:, base + k * 128: base + (k + 1) * 128], rhs_of(k), k == 0, k == nk - 1,
                   [f"W{sl}", rhs_reads(k)])
            bias = 0.0 if bias_key is None else cp(bias_key, j)
            S.op("dve", lambda e, j=j, b=b, bias=bias: e.tensor_scalar(Y[:, j, :], PS[b][:, :], bias, None, op0=ALU.add),
                 reads=[f"PS{b}", "CPP"], writes=[f"Y.{j}"])
            S.op("act", lambda e, j=j, b=b, bias=bias: e.activation(out=c.H[:, j, :], in_=PS[b][:, :], func=AF.Square, bias=bias),
                 reads=[f"PS{b}", "CPP"], writes=[f"{c.p}H.{j}"])
            yield

    def conv3(dst, dst_reg, src, src_reg, wkey, widx):
        S.op("dve", lambda e: e.tensor_scalar(dst, src[0], cp(wkey, widx), None, op0=ALU.mult),
             reads=[src_reg, "CPP"], writes=[dst_reg])
        for k in (1, 2):
            S.op("dve", lambda e, k=k: e.scalar_tensor_tensor(out=dst, in0=src[k], scalar=cp(wkey, widx + k), in1=dst,
                                                             op0=ALU.mult, op1=ALU.add),
                 reads=[src_reg, dst_reg, "CPP"], writes=[dst_reg])

    def mixer_a(c, l, seq_start):
        H = c.H
        sv = [acquire(), acquire(hold=1)]
        nblk = T // 128
        VTS = [(TMPT[:, 4:6, :], ["TMP4", "TMP5"]), (M2[:, 0:2, 0:T], ["M2.0", "M2.1"])]
        lng = CBC[:, bo["lng"]:bo["lng"] + 1024].rearrange("p (a t) -> p a t", a=2)
        lnb = CBC[:, bo["lnb"]:bo["lnb"] + 1024].rearrange("p (a t) -> p a t", a=2)
        for blk in range(nblk):
            vt, vr = VTS[blk % 2]
            for h in range(2):
                b = newbank()
                for k in range(NCH):
                    MM(b, H[:, k, blk * 128:(blk + 1) * 128], SLOTS[sv[h]][:, k * 512:(k + 1) * 512], k == 0, k == NCH - 1,
                       [f"W{sv[h]}", f"{c.p}H.{k}"])
                S.op("act", lambda e, h=h, b=b, vt=vt: e.activation(out=vt[:, h, :], in_=PS[b][:, :], func=AF.Gelu),
                     reads=[f"PS{b}"], writes=[vr[h]])
                S.op("dve", lambda e, h=h, vt=vt, blk=blk: e.bn_stats(out=ST[:, (blk % 2) * 12 + h * 6:(blk % 2) * 12 + (h + 1) * 6], in_=vt[:, h, :]),
                     reads=[vr[h]], writes=[f"ST.{blk % 2}.{h}"])
                yield
            q = blk % 2
            S.op("dve", lambda e, q=q: e.bn_aggr(out=MV[:, q * 4:q * 4 + 2], in_=ST[:, q * 12:q * 12 + 12]),
                 reads=[f"ST.{q}.0", f"ST.{q}.1"], writes=[f"MV{q}"])
            S.op("act", lambda e, q=q: e.activation(out=MV[:, q * 4 + 2:q * 4 + 3], in_=MV[:, q * 4 + 1:q * 4 + 2], func=AF.Sqrt, bias=EPS),
                 reads=[f"MV{q}"], writes=[f"MV{q}s"])
            S.op("dve", lambda e, q=q: e.reciprocal(MV[:, q * 4 + 3:q * 4 + 4], MV[:, q * 4 + 2:q * 4 + 3]), reads=[f"MV{q}s"], writes=[f"MV{q}r"])
            S.op("dve", lambda e, q=q, vt=vt: e.tensor_scalar(vt, vt, MV[:, q * 4:q * 4 + 1], MV[:, q * 4 + 3:q * 4 + 4], op0=ALU.subtract, op1=ALU.mult),
                 reads=vr + [f"MV{q}", f"MV{q}r"], writes=vr)
            S.op("dve", lambda e, vt=vt: e.tensor_tensor(out=vt, in0=vt, in1=lng, op=ALU.mult),
                 reads=vr + ["CBC"], writes=vr)
            S.op("dve", lambda e, blk=blk, vt=vt: e.tensor_tensor(out=VNB[:, blk, :].rearrange("p (a t) -> p a t", a=2), in0=vt, in1=lnb, op=ALU.add),
                 reads=vr + ["CBC"], writes=[f"VNB.{blk}"] + zb_over(blk))
        for j in range(NCH):
            if j % 4 == 0:
                sl = acquire()
            b = newbank()
            for k in range(NCH):
                o = ((j % 4) * 8 + k) * 128
                MM(b, SLOTS[sl][:, o:o + 128], H[:, k, :], k == 0, k == NCH - 1, [f"W{sl}", f"{c.p}H.{k}"])
            S.op("act", lambda e, j=j, b=b: e.activation(out=M1[:, j, 0:T], in_=PS[b][:, :], func=AF.Gelu),
                 reads=[f"PS{b}"], writes=[f"M1.{j}"])
            yield
        for g in range(8):
            b = newbank()
            for blk in range(nblk):
                MM(b, VNB[:, blk, g * 128:(g + 1) * 128], WST[:, g, :], True, True, [f"VNB.{blk}", "WST"],
                   cols=(blk * 128, (blk + 1) * 128))
            t = newtmp(3)
            bs = CBC[:, bo["bs"] + g * 128: bo["bs"] + (g + 1) * 128]
            for blk in range(nblk):
                S.op("dve", lambda e, blk=blk, b=b, t=t, bs=bs: e.tensor_tensor(out=TMP[t][:, blk * 128:(blk + 1) * 128],
                                                                         in0=PS[b][:, blk * 128:(blk + 1) * 128], in1=bs, op=ALU.add),
                     reads=[f"PS{b}", "CBC"], writes=[f"TMP{t}"])
            S.op("dve", lambda e, g=g, t=t: e.tensor_tensor(out=ACTB[:, g, :], in0=TMP[t], in1=M1[:, g, 0:T], op=ALU.mult),
                 reads=[f"TMP{t}", f"M1.{g}"], writes=[f"ACTB.{g}"])
            yield
        yield from outproj(c, NCH, lambda k: ACTB[:, k, :], lambda k: f"ACTB.{k}")

    def mixer_b(c, l, seq_start):
        H = c.H
        CB = c.CB
        cbr = f"{c.p}CB"
        if seq_start:
            S.op("dve", lambda e: e.memset(CB[:, :, :], 0.0), writes=[cbr])
        S.op("dve", lambda e: e.tensor_copy(M1[:, :, 0:16], CB[:, :, :]), reads=[cbr], writes=allr("M1"))
        for j in range(NCH):
            if j % 4 == 0:
                sl = acquire()
            b = newbank()
            for k in range(NCH):
                o = ((j % 4) * 8 + k) * 128
                MM(b, SLOTS[sl][:, o:o + 128], H[:, k, :], k == 0, k == NCH - 1, [f"W{sl}", f"{c.p}H.{k}"])
            S.op("act", lambda e, j=j, b=b: e.activation(out=M1[:, j, 16:16 + T], in_=PS[b][:, :], func=AF.Copy),
                 reads=[f"PS{b}"], writes=[f"M1.{j}"])
            yield
        S.op("dve", lambda e: e.tensor_copy(CB[:, :, :], M1[:, :, T:T + 16]), reads=allr("M1"), writes=[cbr])

        def add(out, a, b_, reads, writes):
            S.op("dve", lambda e: e.tensor_tensor(out=out, in0=a, in1=b_, op=ALU.add), reads=reads, writes=writes)
        r = lambda name, lo, hi: [f"{name}.{j}" for j in range(lo, hi)]
        add(Y[:, 0:2, :], M1[:, 0:2, 16:16 + T], M1[:, 0:2, 15:15 + T], r("M1", 0, 2), r("Y", 0, 2))
        add(M2[:, 2:8, 2:16 + T], M1[:, 2:8, 2:16 + T], M1[:, 2:8, 1:15 + T], r("M1", 2, 8), r("M2", 2, 8))
        add(Y[:, 2:4, :], M2[:, 2:4, 16:16 + T], M2[:, 2:4, 14:14 + T], r("M2", 2, 4), r("Y", 2, 4))
        add(M2[:, 0:4, 4:16 + T], M2[:, 4:8, 4:16 + T], M2[:, 4:8, 2:14 + T], r("M2", 4, 8), r("M2", 0, 4))
        add(Y[:, 4:6, :], M2[:, 0:2, 16:16 + T], M2[:, 0:2, 12:12 + T], r("M2", 0, 2), r("Y", 4, 6))
        add(M2[:, 6:8, 8:16 + T], M2[:, 2:4, 8:16 + T], M2[:, 2:4, 4:12 + T], r("M2", 2, 4), r("M2", 6, 8))
        add(Y[:, 6:8, :], M2[:, 6:8, 16:16 + T], M2[:, 6:8, 8:8 + T], r("M2", 6, 8), r("Y", 6, 8))
        for g, w in enumerate((2, 4, 8, 16)):
            for ch in (2 * g, 2 * g + 1):
                S.op("dve", lambda e, ch=ch, w=w: e.scalar_tensor_tensor(out=ACTB[:, ch, :], in0=Y[:, ch, :], scalar=1.0 / w,
                                                                         in1=M1[:, ch, 16:16 + T], op0=ALU.mult, op1=ALU.subtract),
                     reads=[f"Y.{ch}", f"M1.{ch}"], writes=[f"ACTB.{ch}"])
                if seq_start:
                    t = newtmp(3)
                    S.op("dve", lambda e, ch=ch, g=g, t=t: e.tensor_tensor(out=TMP[t][:, 0:16], in0=Y[:, ch, 0:16],
                                                                         in1=CBC[:, bo["invc"] + g * 16: bo["invc"] + (g + 1) * 16], op=ALU.mult),
                         reads=[f"Y.{ch}", "CBC"], writes=[f"TMP{t}"])
                    S.op("dve", lambda e, ch=ch, t=t: e.tensor_tensor(out=ACTB[:, ch, 0:16], in0=TMP[t][:, 0:16], in1=M1[:, ch, 16:32],
                                                                    op=ALU.subtract),
                         reads=[f"TMP{t}", f"M1.{ch}", f"ACTB.{ch}"], writes=[f"ACTB.{ch}"])
        sl = acquire()
        for g in range(4):
            for dj in range(2):
                b = newbank()
                for ci in range(2):
                    o = ((g * 2 + dj) * 2 + ci) * 128
                    MM(b, SLOTS[sl][:, o:o + 128], ACTB[:, 2 * g + ci, :], ci == 0, ci == 1, [f"W{sl}", f"ACTB.{2 * g + ci}"])
                ch = 2 * g + dj
                S.op("act", lambda e, ch=ch, b=b: e.activation(out=ACTB[:, 8 + ch, :], in_=PS[b][:, :], func=AF.Copy, scale=cp("bsc", ch)),
                     reads=[f"PS{b}", "CPP"], writes=[f"ACTB.{8 + ch}"])
            yield
        yield from outproj(c, NCH, lambda k: ACTB[:, 8 + k, :], lambda k: f"ACTB.{8 + k}")

    def mixer_c(c, l, seq_start):
        H = c.H
        CC = c.CC
        ccr = f"{c.p}CC"
        if seq_start:
            S.op("dve", lambda e: e.memset(CC[:, :, :], 0.0), writes=[ccr])
        S.op("dve", lambda e: e.tensor_copy(ZB[:, :, 0:2], CC[:, :, :]), reads=[ccr], writes=allr("ZB") + allr("VNB", 4))

        def conv(j, sl):
            b = newbank()
            for k in range(3):
                o = 3072 + k * 128
                MM(b, SLOTS[sl][:, o:o + 128], ZB[:, j, k:k + T], k == 0, k == 2, [f"W{sl}", f"ZB.{j}"])
            S.op("dve", lambda e: e.tensor_tensor(out=ACTB[:, j, :], in0=PS[b][:, :], in1=M1[:, j, 0:T], op=ALU.mult),
                 reads=[f"PS{b}", f"M1.{j}"], writes=[f"ACTB.{j}"])

        pend = None
        for j in range(NCH):
            sl = acquire(hold=1 if pend is not None else 0)
            bk = []
            for part in range(3):
                b = newbank()
                bk.append(b)
                for k in range(NCH):
                    o = (part * 8 + k) * 128
                    MM(b, SLOTS[sl][:, o:o + 128], H[:, k, :], k == 0, k == NCH - 1, [f"W{sl}", f"{c.p}H.{k}"])
            bb, bc, bx = bk
            t1 = newtmp()
            S.op("act", lambda e, t1=t1, bx=bx: e.activation(out=TMP[t1], in_=PS[bx][:, :], func=AF.Copy),
                 reads=[f"PS{bx}"], writes=[f"TMP{t1}"])
            S.op("dve", lambda e, j=j, t1=t1, bc=bc: e.tensor_tensor(out=ZB[:, j, 2:2 + T], in0=PS[bc][:, :], in1=TMP[t1], op=ALU.mult),
                 reads=[f"PS{bc}", f"TMP{t1}"], writes=[f"ZB.{j}"])
            S.op("act", lambda e, j=j, bb=bb: e.activation(out=M1[:, j, 0:T], in_=PS[bb][:, :], func=AF.Copy),
                 reads=[f"PS{bb}"], writes=[f"M1.{j}"])
            if pend is not None:
                conv(*pend)
            pend = (j, sl)
            yield
        conv(*pend)
        S.op("dve", lambda e: e.tensor_copy(CC[:, :, :], ZB[:, :, T:T + 2]), reads=allr("ZB"), writes=[ccr])
        yield
        yield from outproj(c, NCH, lambda k: ACTB[:, k, :], lambda k: f"ACTB.{k}")

    def mixer_d(c, l, seq_start):
        H = c.H
        CD = c.CD
        cdr = f"{c.p}CD"
        if seq_start:
            S.op("dve", lambda e: e.memset(CD[:, :, :], 0.0), writes=[cdr])
        S.op("dve", lambda e: e.tensor_copy(ZB[:, :, 0:32], CD[:, :, :]), reads=[cdr], writes=allr("ZB") + allr("VNB", 4))
        for j in range(NCH):
            if j % 2 == 0:
                sl = acquire()
            ba = newbank()
            bg = newbank()
            for part, b in ((0, ba), (1, bg)):
                for k in range(NCH):
                    o = (((j % 2) * 2 + part) * 8 + k) * 128
                    MM(b, SLOTS[sl][:, o:o + 128], H[:, k, :], k == 0, k == NCH - 1, [f"W{sl}", f"{c.p}H.{k}"])
            t1 = newtmp()
            S.op("act", lambda e, j=j, t1=t1, bg=bg: e.activation(out=TMP[t1], in_=PS[bg][:, :], func=AF.Sigmoid, bias=cp("db1", 8 + j)),
                 reads=[f"PS{bg}", "CPP"], writes=[f"TMP{t1}"])
            S.op("dve", lambda e, j=j, t1=t1, ba=ba: e.scalar_tensor_tensor(out=ZB[:, j, 32:32 + T], in0=PS[ba][:, :], scalar=cp("db1", j),
                                                                            in1=TMP[t1], op0=ALU.add, op1=ALU.mult),
                 reads=[f"PS{ba}", f"TMP{t1}", "CPP"], writes=[f"ZB.{j}"])
            yield
        S.op("dve", lambda e: e.tensor_copy(CD[:, :, :], ZB[:, :, T:T + 32]), reads=allr("ZB"), writes=[cdr])
        for j in range(NCH):
            sl = acquire()
            b = newbank()
            for k in range(31):
                MM(b, SLOTS[sl][:, k * 128:(k + 1) * 128], ZB[:, j, 2 + k:2 + k + T], k == 0, k == 30, [f"W{sl}", f"ZB.{j}"])
            S.op("act", lambda e, j=j, b=b: e.activation(out=M2[:, j, 0:T], in_=PS[b][:, :], func=AF.Identity, bias=cp("dcb", j)),
                 reads=[f"PS{b}", "CPP"], writes=[f"M2.{j}"])
            S.op("act", lambda e, j=j, b=b: e.activation(out=H[:, j, :], in_=PS[b][:, :], func=AF.Identity, bias=cp("dcb", j)),
                 reads=[f"PS{b}", "CPP"], writes=[f"{c.p}H.{j}"])
            S.op("act", lambda e, j=j, b=b: e.activation(out=ACTB[:, 8 + j, :], in_=PS[b][:, :], func=AF.Square, bias=cp("dcb", j)),
                 reads=[f"PS{b}", "CPP"], writes=[f"ACTB.{8 + j}"])
            yield
        bm = newbank()
        for j in range(NCH):
            MM(bm, ONES[:, :], H[:, j, :], j == 0, j == NCH - 1, ["ONES", f"{c.p}H.{j}"])
        bq = newbank()
        for j in range(NCH):
            MM(bq, ONES[:, :], ACTB[:, 8 + j, :], j == 0, j == NCH - 1, ["ONES", f"ACTB.{8 + j}"])
        tm = 4
        tv = 5
        S.op("act", lambda e: e.activation(out=TMP[tm], in_=PS[bm][:, :], func=AF.Copy), reads=[f"PS{bm}"], writes=[f"TMP{tm}"])
        S.op("dve", lambda e: e.tensor_tensor(out=TMP[tv], in0=TMP[tm], in1=TMP[tm], op=ALU.mult),
             reads=[f"TMP{tm}"], writes=[f"TMP{tv}"])
        S.op("dve", lambda e: e.tensor_tensor(out=TMP[tv], in0=PS[bq][:, :], in1=TMP[tv], op=ALU.subtract),
             reads=[f"PS{bq}", f"TMP{tv}"], writes=[f"TMP{tv}"])
        S.op("act", lambda e: e.activation(out=TMP[tv], in_=TMP[tv], func=AF.Sqrt, bias=EPS), reads=[f"TMP{tv}"], writes=[f"TMP{tv}"])
        S.op("dve", lambda e: e.reciprocal(TMP[tv], TMP[tv]), reads=[f"TMP{tv}"], writes=[f"TMP{tv}"])
        yield
        for j in range(NCH):
            S.op("dve", lambda e, j=j: e.tensor_tensor(out=M2[:, j, 0:T], in0=M2[:, j, 0:T], in1=TMP[tm], op=ALU.subtract),
                 reads=[f"M2.{j}", f"TMP{tm}"], writes=[f"M2.{j}"])
            S.op("dve", lambda e, j=j: e.tensor_tensor(out=M2[:, j, 0:T], in0=M2[:, j, 0:T], in1=TMP[tv], op=ALU.mult),
                 reads=[f"M2.{j}", f"TMP{tv}"], writes=[f"M2.{j}"])
            S.op("act", lambda e, j=j: e.activation(out=ACTB[:, j, :], in_=M2[:, j, 0:T], func=AF.Silu, scale=cp("dlg", j), bias=cp("dlb", j)),
                 reads=[f"M2.{j}", "CPP"], writes=[f"ACTB.{j}"])
        yield from outproj(c, NCH, lambda k: ACTB[:, k, :], lambda k: f"ACTB.{k}", bias_key="db2")

    def ffn(c, l, seq_start):
        H = c.H
        CF = c.CF[l]
        cfr = f"{c.p}CF{l}"
        if seq_start:
            S.op("act", lambda e: e.activation(out=CF[:, :, :], in_=ZERO2[:, 0:88].rearrange("p (a b) -> p a b", b=2), func=AF.Copy),
                 reads=["ZERO2"], writes=[cfr])

        def conv(p, sl, dbase=2048):
            par = p % 2
            src = []
            for gu in range(2):
                chunk = p if gu == 0 else NPAIR + p
                if dve_conv(gu, p):
                    a = AS[gu][par]
                    an = f"AS{gu}{par}"
                    buf = GB[gu][par]
                    bn = f"GB{gu}{par}"
                    for k in (1, 0):
                        S.op("dve", lambda e, k=k, a=a, buf=buf, chunk=chunk: e.scalar_tensor_tensor(
                            out=a[:, :], in0=buf[:, k:k + T], scalar=cp("fcw", (l * 44 + chunk) * 3 + k), in1=a[:, :],
                            op0=ALU.mult, op1=ALU.add), reads=[bn, bn + "x", an, "CPP"], writes=[an])
                    src.append((a[:, :], an))
                else:
                    assert gu == 0
                    b = newbank()
                    for k in range(3):
                        o = dbase + k * 128
                        MM(b, SLOTS[sl][:, o:o + 128], GB[gu][par][:, k:k + T], k == 0, k == 2, [f"W{sl}", f"GB{gu}{par}", f"GB{gu}{par}x"])
                    src.append((PS[b][:, :], f"PS{b}"))
            t = newtmp()
            S.op("act", lambda e: e.activation(out=TMP[t], in_=src[0][0], func=AF.Silu),
                 reads=[src[0][1]], writes=[f"TMP{t}"])
            return (p, t, src[1])

        def gate(p, t, u):
            S.op("dve", lambda e: e.tensor_tensor(out=ACTB[:, p, :], in0=u[0], in1=TMP[t], op=ALU.mult),
                 reads=[f"TMP{t}", u[1]], writes=[f"ACTB.{p}"])

        pend = None
        pend2 = None
        for p in range(NPAIR):
            if pend2 is not None:
                gate(*pend2)
                pend2 = None
            sl = acquire()
            par = p % 2
            banks = []
            for gu in range(2):
                b = newbank()
                banks.append(b)
                for k in range(NCH):
                    o = (gu * 8 + k) * 128
                    MM(b, SLOTS[sl][:, o:o + 128], H[:, k, :], k == 0, k == NCH - 1, [f"W{sl}", f"{c.p}H.{k}"])
            for gu in range(2):
                chunk = p if gu == 0 else NPAIR + p
                buf = GB[gu][par]
                bn = f"GB{gu}{par}"
                b = banks[gu]
                S.op("act", lambda e, buf=buf, chunk=chunk: e.activation(out=buf[:, 0:2], in_=CF[:, chunk, :], func=AF.Copy),
                     reads=[cfr], writes=[bn])
                S.op("act", lambda e, buf=buf, b=b: e.activation(out=buf[:, 2:2 + T], in_=PS[b][:, :], func=AF.Copy),
                     reads=[f"PS{b}"], writes=[bn + "x"])
                S.op("act", lambda e, b=b, chunk=chunk: e.activation(out=CF[:, chunk, :], in_=PS[b][:, T - 2:T], func=AF.Copy),
                     reads=[f"PS{b}"], writes=[cfr])
                if dve_conv(gu, p):
                    a = AS[gu][par]
                    S.op("act", lambda e, a=a, b=b, chunk=chunk: e.activation(out=a[:, :], in_=PS[b][:, :], func=AF.Copy,
                                                                              scale=cp("fcw", (l * 44 + chunk) * 3 + 2)),
                         reads=[f"PS{b}", "CPP"], writes=[f"AS{gu}{par}"])
            if pend is not None:
                pend2 = conv(pend, sl)
            pend = p
            yield
        if pend2 is not None:
            gate(*pend2)
        sl0 = acquire()
        gate(*conv(pend, sl0, NPAIR * 128))
        yield
        yield from outproj(c, NPAIR, lambda k: ACTB[:, k, :], lambda k: f"ACTB.{k}", first_slot=sl0)

    mixers = [mixer_a, mixer_b, mixer_c, mixer_d]
    nsub = len(sub_list)
    stores = {}

    def body(c, ti, si):
        l, k = sub_list[si]
        seq_start = ((c.tok0 + ti * T) % seq) == 0
        def tagged(g, name):
            for _ in g:
                yield
                S.tag = name
        if k == 0:
            S.tag = f"mix{l % 4}"
            return tagged(mixers[l % 4](c, l, seq_start), f"mix{l % 4}")
        S.tag = "ffn"
        return tagged(ffn(c, l, seq_start), "ffn")

    def gpre(si):
        l, k = sub_list[si]
        return (0 if k == 0 else 2) * 4 + l

    def gpost(si):
        l, k = sub_list[si]
        return (1 if k == 0 else 3) * 4 + l

    def load(c, ti):
        n0 = c.tok0 + ti * T
        S.op("pool", lambda e: e.dma_start(out=c.X[:, :, :], in_=xT[:, :, n0:n0 + T]), writes=xr(c), dma=f"xl{c.p}")

    def store(c, ti):
        n0 = c.tok0 + ti * T
        stores[c.p] = S.op("pool", lambda e: e.dma_start(out=oT[:, :, n0:n0 + T], in_=c.X[:, :, :]), reads=xr(c), dma=f"xs{c.p}")

    BODY_STEPS = {("mix", 0): 32, ("mix", 1): 20, ("mix", 2): 17, ("mix", 3): 25, ("ffn", 0): 31}

    def chain(c, ti, si):
        if c is CA:
            l2, k2 = sub_list[si]
        else:
            l2, k2 = sub_list[(si + 1) % nsub]
        steps = BODY_STEPS[("ffn", 0)] if k2 == 1 else BODY_STEPS[("mix", l2 % 4)]
        per = 1 if steps >= 26 else 2
        yield from post_stage(c, gpost(si), per)
        if si == nsub - 1:
            store(c, ti)
            if ti + 1 >= ntiles:
                return
            load(c, ti + 1)
            yield
            yield from pre_sq(c)
            for _ in range(CHAIN_SLACK):
                yield
            yield from pre_norm(c, gpre(0), per)
        else:
            yield
            yield from pre_sq(c)
            for _ in range(CHAIN_SLACK):
                yield
            yield from pre_norm(c, gpre(si + 1), per)

    def first_pre(c):
        pre_stage(c, gpre(0))
        yield

    def interleave(bd, ch):
        for _ in bd:
            if ch is not None:
                try:
                    next(ch)
                except StopIteration:
                    ch = None
        if ch is not None:
            for _ in ch:
                pass

    load(CA, 0)
    load(CBX, 0)
    pre_stage(CA, gpre(0))
    chB = first_pre(CBX)
    for ti in range(ntiles):
        for si in range(nsub):
            interleave(body(CA, ti, si), chB)
            chA = chain(CA, ti, si)
            interleave(body(CBX, ti, si), chA)
            chB = chain(CBX, ti, si)
    for _ in chB:
        pass
    S.final_wait("pool", [stores["A"], stores["B"]])
    S.emit()
    return nc, S


def layout_x(xc):
    n = xc.shape[0]
    return np.ascontiguousarray(xc.T.reshape(NCH, 128, n).transpose(1, 0, 2))


def unlayout_x(o):
    n = o.shape[2]
    return np.ascontiguousarray(o.transpose(1, 0, 2).reshape(D, n).T)


def kernel(**inputs):
    inputs = {k: np.asarray(v) for k, v in inputs.items()}
    x = inputs["x"]
    B, S_, _ = x.shape
    ncores = 8
    per = B // ncores
    ntok = per * S_
    wflat, cpp, cbc, wst = pack_host(inputs)
    nc, _ = build_program(ntok)
    in_maps = []
    for c in range(ncores):
        xc = x[c * per:(c + 1) * per].reshape(ntok, D)
        in_maps.append({"xT": layout_x(xc), "wflat": wflat, "cpp": cpp, "cbc": cbc, "wst": wst})
    res = run_bass_kernel_spmd(nc, in_maps, core_ids=list(range(ncores)))
    out = np.empty((B, S_, D), np.float32)
    for c in range(ncores):
        out[c * per:(c + 1) * per] = unlayout_x(res.results[c]["oT"]).reshape(per, S_, D)
    return out
```

```python
import numpy as np
import concourse.bass as bass
import concourse.mybir as mybir
from concourse.bass_utils import run_bass_kernel_spmd

F32 = mybir.dt.float32
BF16 = mybir.dt.bfloat16
AF = mybir.ActivationFunctionType
ALU = mybir.AluOpType

D = 1024
NCH = 8
DFF = 2816
NPAIR = 22
SEQ = 4096
DEPTH = 4
EPS = 1e-6
RING = 4
G_DVE_MOD = 2
G_DVE_CNT = 0
CHAIN_SLACK = 3
SLOT = 4096

ENGS = ("pe", "act", "dve", "pool", "sp")


class Sched:
    def __init__(self, nc):
        self.nc = nc
        self.streams = {e: [] for e in ENGS}
        self.prog = {e: nc.alloc_semaphore(name=f"prog_{e}") for e in ENGS}
        self.count = {e: 0 for e in ENGS}
        self.seen = {e: {} for e in ENGS}
        self.sems = {("prog", e): self.prog[e] for e in ENGS}
        self.dma_count = {}
        self.reg = {}
        self.tag = ""

    def dma_sem(self, name):
        k = ("dma", name)
        if k not in self.sems:
            self.sems[k] = self.nc.alloc_semaphore(name=f"dma_{name}")
            self.dma_count[name] = 0
        return k

    def _need(self, eng, tok, waits):
        if tok is None:
            return
        k, v = tok
        if eng == "pe" and k == ("prog", "pe"):
            return
        if self.seen[eng].get(k, 0) >= v:
            return
        waits[k] = max(waits.get(k, 0), v)

    def op(self, eng, fn, reads=(), writes=(), dma=None, inc=True):
        waits = {}
        for r in reads:
            w, rd = self.reg.get(r, (None, {}))
            self._need(eng, w, waits)
            if r.startswith("PS") and eng in ("act", "dve"):
                other = ("prog", "dve" if eng == "act" else "act")
                if other in rd:
                    self._need(eng, (other, rd[other]), waits)
        for r in writes:
            w, rd = self.reg.get(r, (None, {}))
            self._need(eng, w, waits)
            for k, v in rd.items():
                if k == ("prog", eng):
                    continue
                self._need(eng, (k, v), waits)
        for k, v in waits.items():
            self.seen[eng][k] = v
        if dma is None and not inc:
            assert eng == "pe"
            tok = (("prog", eng), self.count[eng] + 1)
            inc = None
        elif dma is None:
            self.count[eng] += 1
            tok = (("prog", eng), self.count[eng])
            inc = (self.prog[eng], 1)
        else:
            k = self.dma_sem(dma)
            self.dma_count[dma] += 16
            tok = (k, self.dma_count[dma])
            inc = (self.sems[k], 16)
        self.streams[eng].append((list(waits.items()), fn, inc, self.tag))
        for r in reads:
            w, rd = self.reg.get(r, (None, {}))
            rd = dict(rd)
            rd[tok[0]] = max(rd.get(tok[0], 0), tok[1])
            self.reg[r] = (w, rd)
        for r in writes:
            self.reg[r] = (tok, {})
        return tok

    def final_wait(self, eng, toks):
        waits = {}
        for t in toks:
            self._need(eng, t, waits)
        self.streams[eng].append((list(waits.items()), None, None, "final"))

    def replay(self, eng, e):
        for waits, fn, inc, _tag in self.streams[eng]:
            for k, v in waits:
                e.wait_ge(self.sems[k], v)
            if fn is not None:
                ins = fn(e)
                if inc is not None:
                    ins.then_inc(inc[0], inc[1])

    def emit(self):
        with self.nc.Block() as block:
            @block.tensor
            def _(e):
                self.replay("pe", e)

            @block.scalar
            def _(e):
                self.replay("act", e)

            @block.vector
            def _(e):
                self.replay("dve", e)

            @block.gpsimd
            def _(e):
                self.replay("pool", e)

            @block.sync
            def _(e):
                self.replay("sp", e)


def lin_block(W, col0, ncols=128):
    K = W.shape[0]
    return W[:, col0:col0 + ncols].reshape(K // 128, 128, ncols).transpose(1, 0, 2)


def diag_blocks(vecs):
    out = np.zeros((128, len(vecs) * 128), np.float32)
    idx = np.arange(128)
    for i, v in enumerate(vecs):
        out[idx, i * 128 + idx] = v
    return out


def layer_pieces(l, inp):
    m = l % 4
    out = []

    def add(tag, F, fn):
        out.append((tag, F, fn))

    def lin_piece(W_fn, cols, nk, extra_F=0, extra_fn=None):
        F = len(cols) * nk * 128 + extra_F
        def fn():
            W = W_fn()
            parts = [lin_block(W, c).reshape(128, nk * 128) for c in cols]
            if extra_fn is not None:
                parts.append(extra_fn())
            return np.concatenate(parts, axis=1)
        return F, fn

    if m == 0:
        Wi = lambda: inp["a_w_in"][0]
        for h in range(2):
            add(f"a_v{h}", 8 * 512, (lambda h=h: lin_block(Wi(), 1024 + h * 512, 512).reshape(128, 8 * 512)))
        for h in range(2):
            F, fn = lin_piece(Wi, [(h * 4 + jj) * 128 for jj in range(4)], 8)
            add(f"a_u{h}", F, fn)
        Wo = lambda: inp["a_w_out"][0]
    elif m == 1:
        Wi = lambda: inp["b_w_in"][0]
        for h in range(2):
            F, fn = lin_piece(Wi, [(h * 4 + jj) * 128 for jj in range(4)], 8)
            add(f"b_in{h}", F, fn)

        def grp():
            Wg = inp["b_w_grp"][0]
            blocks = []
            for g in range(4):
                for dj in range(2):
                    blocks.append(lin_block(Wg[g], dj * 128).reshape(128, 2 * 128))
            return np.concatenate(blocks, axis=1)
        add("b_grp", 4 * 2 * 2 * 128, grp)
        Wo = lambda: inp["b_w_out"][0]
    elif m == 2:
        Wi = lambda: inp["c_w_in"][0]
        for j in range(8):
            ex = lambda j=j: diag_blocks([inp["c_conv_w"][0][k, j * 128:(j + 1) * 128] for k in range(3)])
            F, fn = lin_piece(Wi, [j * 128, 1024 + j * 128, 2048 + j * 128], 8, 3 * 128, ex)
            add(f"c_in{j}", F, fn)
        Wo = lambda: inp["c_w_out"][0]
    else:
        Wi = lambda: inp["d_w1"][0]
        for q in range(4):
            cols = []
            for pp in range(2):
                j = q * 2 + pp
                cols += [j * 128, 1024 + j * 128]
            F, fn = lin_piece(Wi, cols, 8)
            add(f"d_in{q}", F, fn)
        for j in range(8):
            add(f"d_cv{j}", 31 * 128, (lambda j=j: diag_blocks([inp["d_conv_w"][0][k, j * 128:(j + 1) * 128] for k in range(31)])))
        Wo = lambda: inp["d_w2"][0]
    for h in range(2):
        F, fn = lin_piece(Wo, [(h * 4 + jj) * 128 for jj in range(4)], 8)
        add(f"m_out{h}", F, fn)
    Wu = lambda: inp["f_w_up"][l]

    def gdiag(p):
        cw = inp["f_conv_w"][l]
        if p < 0:
            return np.zeros((128, 3 * 128), np.float32)
        return diag_blocks([cw[k, p * 128:(p + 1) * 128] for k in range(3)])
    for p in range(NPAIR):
        F, fn = lin_piece(Wu, [p * 128, DFF + p * 128], 8, 3 * 128, (lambda p=p: gdiag(p - 1)))
        add(f"f_up{p}", F, fn)
    Wd = lambda: inp["f_w_down"][l]
    for j in range(8):
        if j == 0:
            F, fn = lin_piece(Wd, [j * 128], NPAIR, 3 * 128, (lambda: gdiag(NPAIR - 1)))
        else:
            F, fn = lin_piece(Wd, [j * 128], NPAIR)
        add(f"f_dn{j}", F, fn)
    return out


def piece_table():
    tab = []
    off = 0
    for l in range(DEPTH):
        for tag, F, _ in layer_pieces(l, None):
            tab.append((l, tag, off, F))
            off += F
    return tab, off


def cpp_layout():
    off = {}
    n = 0
    def add(k, w):
        nonlocal n
        off[k] = n
        n += w
    add("ng", 16 * 8)
    add("fcw", 4 * 44 * 3)
    add("bsc", 8)
    add("ccw", 8 * 3)
    add("db1", 16)
    add("dcw", 8 * 31)
    add("dcb", 8)
    add("dlg", 8)
    add("dlb", 8)
    add("db2", 8)
    return off, n


def cbc_layout():
    off = {}
    n = 0
    def add(k, w):
        nonlocal n
        off[k] = n
        n += w
    add("bs", 8 * 128)
    add("lng", 1024)
    add("lnb", 1024)
    add("invc", 4 * 16)
    return off, n


def colvec(v):
    return np.ascontiguousarray(v.reshape(-1, 128).T)


def pack_host(inp):
    tab, WF = piece_table()
    wflat = np.empty((128, WF), np.float32)
    i = 0
    for l in range(DEPTH):
        for tag, F, fn in layer_pieces(l, inp):
            _, _, off, F2 = tab[i]
            wflat[:, off:off + F] = fn()
            i += 1
    co, ncp = cpp_layout()
    cpp = np.zeros((128, ncp), np.float32)
    for t, key in enumerate(["norm_mix_pre", "norm_mix_post", "norm_ffn_pre", "norm_ffn_post"]):
        for l in range(DEPTH):
            o = co["ng"] + (t * 4 + l) * 8
            cpp[:, o:o + 8] = colvec(inp[key][l])
    for l in range(DEPTH):
        cw = inp["f_conv_w"][l]
        for k in range(3):
            cv = colvec(cw[k])
            for c in range(44):
                cpp[:, co["fcw"] + (l * 44 + c) * 3 + k] = cv[:, c]
    cpp[:, co["bsc"]:co["bsc"] + 8] = colvec(inp["b_scale"][0])
    for k in range(3):
        cv = colvec(inp["c_conv_w"][0][k])
        for j in range(8):
            cpp[:, co["ccw"] + j * 3 + k] = cv[:, j]
    cpp[:, co["db1"]:co["db1"] + 16] = colvec(inp["d_b1"][0])
    for k in range(31):
        cv = colvec(inp["d_conv_w"][0][k])
        for j in range(8):
            cpp[:, co["dcw"] + j * 31 + k] = cv[:, j]
    cpp[:, co["dcb"]:co["dcb"] + 8] = colvec(inp["d_conv_b"][0])
    cpp[:, co["dlg"]:co["dlg"] + 8] = colvec(inp["d_ln_g"][0])
    cpp[:, co["dlb"]:co["dlb"] + 8] = colvec(inp["d_ln_b"][0])
    cpp[:, co["db2"]:co["db2"] + 8] = colvec(inp["d_b2"][0])
    bo, nbc = cbc_layout()
    cbc = np.zeros((128, nbc), np.float32)
    cbc[:, bo["bs"]:bo["bs"] + 1024] = inp["a_b_s"][0].reshape(1, 1024)
    cbc[:, bo["lng"]:bo["lng"] + 1024] = inp["a_ln_g"][0].reshape(1, 1024)
    cbc[:, bo["lnb"]:bo["lnb"] + 1024] = inp["a_ln_b"][0].reshape(1, 1024)
    for g, w in enumerate((2, 4, 8, 16)):
        for t in range(16):
            cbc[:, bo["invc"] + g * 16 + t] = 1.0 / min(t + 1, w)
    wst = np.ascontiguousarray(inp["a_w_s"][0].transpose(2, 0, 1)).reshape(128, 1024)
    return wflat, cpp, cbc, wst


class Ctx:
    pass


def build_program(ntok, T=512, layers=(0, 1, 2, 3), seq=SEQ):
    nc = bass.Bass("TRN2", target_bir_lowering=False)
    tab, WF = piece_table()
    co, ncp = cpp_layout()
    bo, nbc = cbc_layout()
    xT = nc.dram_tensor("xT", [128, NCH, ntok], F32, kind="ExternalInput").ap()
    wflat = nc.dram_tensor("wflat", [128, WF], F32, kind="ExternalInput").ap()
    cppd = nc.dram_tensor("cpp", [128, ncp], F32, kind="ExternalInput").ap()
    cbcd = nc.dram_tensor("cbc", [128, nbc], F32, kind="ExternalInput").ap()
    wstd = nc.dram_tensor("wst", [128, 1024], F32, kind="ExternalInput").ap()
    oT = nc.dram_tensor("oT", [128, NCH, ntok], F32, kind="ExternalOutput").ap()
    wbf = nc.dram_tensor("wbf", [128, WF], BF16, kind="Internal").ap()

    S = Sched(nc)
    sb = nc.alloc_sbuf_tensor
    HW = T + 32
    Y = sb("Y", [128, NCH, T], F32)
    ACTB = sb("ACTB", [128, NPAIR, T], BF16)
    M1 = sb("M1", [128, NCH, HW], F32)
    M2 = sb("M2", [128, NCH, HW], F32)
    NTMP = 6
    TMPT = sb("TMP", [128, NTMP, T], F32)
    TMP = [TMPT[:, i, :] for i in range(NTMP)]
    VT = TMPT[:, 4:6, :].rearrange("p a t -> p (a t)")
    VTR = ["TMP4", "TMP5"]
    def dve_conv(gu, p):
        return gu == 1 or (p % G_DVE_MOD) < G_DVE_CNT
    GB = [[sb(f"GB{g}{i}", [128, T + 2], BF16) for i in range(2)] for g in range(2)]
    AS = [[sb(f"AS{g}{i}", [128, T], F32) for i in range(2)] for g in range(2)]
    ZV = sb("ZV", [128, NCH * HW], BF16)
    ZB = ZV[:, :].rearrange("p (c w) -> p c w", w=HW)
    SLOTS = [sb(f"WS{i}", [128, SLOT], BF16) for i in range(RING)]
    CPP = sb("CPP", [128, ncp], F32)
    CBC = sb("CBC", [128, nbc], F32)
    WST = sb("WST", [128, 8, 128], BF16)
    ONES = sb("ONES", [128, 128], BF16)
    ZERO2 = sb("ZERO2", [128, 88], BF16)
    VNB = ZV[:, 0:4096].rearrange("p (b c) -> p b c", c=1024)

    def zb_over(blk):
        return [f"ZB.{j}" for j in range((1024 * blk) // HW, min(NCH - 1, (1024 * blk + 1023) // HW) + 1)]

    def vnb_over(j):
        return [f"VNB.{b}" for b in range((HW * j) // 1024, min(3, (HW * j + HW - 1) // 1024) + 1)]
    ST = sb("ST", [128, 24], F32)
    MV = sb("MV", [128, 8], F32)
    PS = [nc.alloc_psum_tensor(f"PS{i}", [128, 512], F32) for i in range(8)]
    print("sbuf remaining before streams", nc.sbuf_bytes_remaining)

    half = ntok // 2
    assert half % seq == 0 or half == seq
    ctxs = []
    for pi, pfx in enumerate("AB"):
        c = Ctx()
        c.p = pfx
        c.tok0 = pi * half
        c.X = sb(f"X{pfx}", [128, NCH, T], F32)
        c.H = sb(f"H{pfx}", [128, NCH, T], BF16)
        c.CF = [sb(f"CF{pfx}{l}", [128, 44, 2], BF16) for l in range(DEPTH)]
        c.CB = sb(f"CB{pfx}", [128, NCH, 16], F32)
        c.CC = sb(f"CC{pfx}", [128, NCH, 2], BF16)
        c.CD = sb(f"CD{pfx}", [128, NCH, 32], BF16)
        ctxs.append(c)
    CA, CBX = ctxs
    ntiles = half // T

    state = {"bank": 0, "tmp": 0}

    def newbank():
        b = state["bank"]
        state["bank"] = (b + 1) % 8
        return b

    def newtmp(n=3):
        i = state["tmp"] % n
        state["tmp"] = (i + 1) % n
        return i

    def cp(key, idx):
        o = co[key] + idx
        return CPP[:, o:o + 1]

    def xr(c, n=NCH):
        return [f"{c.p}X.{j}" for j in range(n)]

    def hr(c, n=NCH):
        return [f"{c.p}H.{j}" for j in range(n)]

    def allr(name, n=NCH):
        return [f"{name}.{j}" for j in range(n)]

    sub_list = [(l, k) for l in layers for k in (0, 1)]
    def sub_pieces(l, k):
        out = []
        for pi, (pl, tag, off, F) in enumerate(tab):
            if pl == l and (tag.startswith("f_") == (k == 1)):
                out.append((pi, off, F))
        return out
    glob = []
    for ti in range(ntiles):
        for (l, k) in sub_list:
            for si in range(2):
                for (pi, off, F) in sub_pieces(l, k):
                    glob.append((pi, off, F, ti == 0 and si == 0))
    ring = {"issued": 0}

    def ring_need(lo):
        while ring["issued"] < min(len(glob), lo + RING):
            q = ring["issued"]
            pi, off, F, first = glob[q]
            sl = q % RING
            if first:
                S.op("pool", lambda e, sl=sl, off=off, F=F: e.dma_start(out=SLOTS[sl][:, 0:F], in_=wflat[:, off:off + F]),
                     writes=[f"W{sl}"], dma=f"w{sl}")
                S.op("sp", lambda e, sl=sl, off=off, F=F: e.dma_start(out=wbf[:, off:off + F], in_=SLOTS[sl][:, 0:F]),
                     reads=[f"W{sl}"], writes=[f"WBF{pi}"], dma=f"wb{q % 4}")
            else:
                S.op("sp", lambda e, sl=sl, off=off, F=F: e.dma_start(out=SLOTS[sl][:, 0:F], in_=wbf[:, off:off + F]),
                     reads=[f"WBF{pi}"], writes=[f"W{sl}"], dma=f"w{sl}")
            ring["issued"] += 1

    piece_ctr = {"n": 0}

    def acquire(hold=0):
        s = piece_ctr["n"]
        piece_ctr["n"] += 1
        ring_need(s - hold)
        return s % RING

    def MM(b, lhsT, rhs, start, stop, reads, cols=None):
        out = PS[b][:, :] if cols is None else PS[b][:, cols[0]:cols[1]]
        S.op("pe", lambda e: e.matmul(out, lhsT, rhs, start=start, stop=stop), reads=reads, writes=[f"PS{b}"], inc=stop)

    S.op("sp", lambda e: e.dma_start(out=CPP[:, :], in_=cppd[:, :]), writes=["CPP"], dma="c")
    S.op("sp", lambda e: e.dma_start(out=CBC[:, :], in_=cbcd[:, :]), writes=["CBC"], dma="c")
    S.op("sp", lambda e: e.dma_start(out=VT, in_=wstd[:, :]), writes=VTR, dma="c")
    S.op("dve", lambda e: e.memset(ONES[:, :], 1.0 / 1024.0), writes=["ONES"])
    S.op("dve", lambda e: e.memset(ZERO2[:, :], 0.0), writes=["ZERO2"])
    S.op("dve", lambda e: e.tensor_copy(WST[:, :, :], VT.rearrange("p (g q) -> p g q", g=8)), reads=VTR, writes=["WST"])
    S.op("dve", lambda e: e.memset(WST[64:128, :, 0:64], 0.0), reads=["WST"], writes=["WST"])

    def rstd_from_bank(b, dst):
        S.op("act", lambda e: e.activation(out=TMP[dst], in_=PS[b][:, :], func=AF.Sqrt, bias=EPS),
             reads=[f"PS{b}"], writes=[f"TMP{dst}"])
        S.op("dve", lambda e: e.reciprocal(TMP[dst], TMP[dst]), reads=[f"TMP{dst}"], writes=[f"TMP{dst}"])

    def pre_sq(c):
        for j in range(NCH):
            S.tag = "pre"
            S.op("act", lambda e, j=j: e.activation(out=c.H[:, j, :], in_=c.X[:, j, :], func=AF.Square),
                 reads=[f"{c.p}X.{j}"], writes=[f"{c.p}H.{j}"])
            if j % 4 == 3:
                yield

    def pre_norm(c, gidx, per=2):
        S.tag = "pre"
        b = newbank()
        for j in range(NCH):
            MM(b, ONES[:, :], c.H[:, j, :], j == 0, j == NCH - 1, ["ONES", f"{c.p}H.{j}"])
        r = 3
        rstd_from_bank(b, r)
        yield
        for j in range(NCH):
            S.tag = "pre"
            S.op("dve", lambda e, j=j: e.scalar_tensor_tensor(out=c.H[:, j, :], in0=c.X[:, j, :], scalar=cp("ng", gidx * 8 + j),
                                                             in1=TMP[r], op0=ALU.mult, op1=ALU.mult),
                 reads=[f"{c.p}X.{j}", f"TMP{r}", "CPP"], writes=[f"{c.p}H.{j}"])
            if j % per == per - 1:
                yield

    def pre_stage(c, gidx):
        for _ in pre_sq(c):
            pass
        for _ in pre_norm(c, gidx):
            pass

    def post_stage(c, gidx, per=2):
        S.tag = "post"
        b = newbank()
        for j in range(NCH):
            MM(b, ONES[:, :], c.H[:, j, :], j == 0, j == NCH - 1, ["ONES", f"{c.p}H.{j}"])
        r = 3
        rstd_from_bank(b, r)
        yield
        for j in range(NCH):
            S.tag = "post"
            S.op("dve", lambda e, j=j: e.scalar_tensor_tensor(out=Y[:, j, :], in0=Y[:, j, :], scalar=cp("ng", gidx * 8 + j),
                                                             in1=TMP[r], op0=ALU.mult, op1=ALU.mult),
                 reads=[f"Y.{j}", f"TMP{r}", "CPP"], writes=[f"Y.{j}"])
            S.op("dve", lambda e, j=j: e.tensor_tensor(out=c.X[:, j, :], in0=c.X[:, j, :], in1=Y[:, j, :], op=ALU.add),
                 reads=[f"{c.p}X.{j}", f"Y.{j}"], writes=[f"{c.p}X.{j}"])
            if j % per == per - 1:
                yield

    def outproj(c, nk, rhs_of, rhs_reads, bias_key=None, first_slot=None):
        for j in range(NCH):
            S.tag = "outproj"
            if nk == NCH:
                if j % 4 == 0:
                    sl = acquire()
                base = (j % 4) * nk * 128
            else:
                sl = first_slot if (j == 0 and first_slot is not None) else acquire()
                base = 0
            b = newbank()
            for k in range(nk):
                MM(b, SLOTS[sl][:, base + k * 128: base + (k + 1) * 128], rhs_of(k), k == 0, k == nk - 1,
                   [f"W{sl}", rhs_reads(k)])
            bias = 0.0 if bias_key is None else cp(bias_key, j)
            S.op("dve", lambda e, j=j, b=b, bias=bias: e.tensor_scalar(Y[:, j, :], PS[b][:, :], bias, None, op0=ALU.add),
                 reads=[f"PS{b}", "CPP"], writes=[f"Y.{j}"])
            S.op("act", lambda e, j=j, b=b, bias=bias: e.activation(out=c.H[:, j, :], in_=PS[b][:, :], func=AF.Square, bias=bias),
                 reads=[f"PS{b}", "CPP"], writes=[f"{c.p}H.{j}"])
            yield

    def conv3(dst, dst_reg, src, src_reg, wkey, widx):
        S.op("dve", lambda e: e.tensor_scalar(dst, src[0], cp(wkey, widx), None, op0=ALU.mult),
             reads=[src_reg, "CPP"], writes=[dst_reg])
        for k in (1, 2):
            S.op("dve", lambda e, k=k: e.scalar_tensor_tensor(out=dst, in0=src[k], scalar=cp(wkey, widx + k), in1=dst,
                                                             op0=ALU.mult, op1=ALU.add),
                 reads=[src_reg, dst_reg, "CPP"], writes=[dst_reg])

    def mixer_a(c, l, seq_start):
        H = c.H
        sv = [acquire(), acquire(hold=1)]
        nblk = T // 128
        VTS = [(TMPT[:, 4:6, :], ["TMP4", "TMP5"]), (M2[:, 0:2, 0:T], ["M2.0", "M2.1"])]
        lng = CBC[:, bo["lng"]:bo["lng"] + 1024].rearrange("p (a t) -> p a t", a=2)
        lnb = CBC[:, bo["lnb"]:bo["lnb"] + 1024].rearrange("p (a t) -> p a t", a=2)
        for blk in range(nblk):
            vt, vr = VTS[blk % 2]
            for h in range(2):
                b = newbank()
                for k in range(NCH):
                    MM(b, H[:, k, blk * 128:(blk + 1) * 128], SLOTS[sv[h]][:, k * 512:(k + 1) * 512], k == 0, k == NCH - 1,
                       [f"W{sv[h]}", f"{c.p}H.{k}"])
                S.op("act", lambda e, h=h, b=b, vt=vt: e.activation(out=vt[:, h, :], in_=PS[b][:, :], func=AF.Gelu),
                     reads=[f"PS{b}"], writes=[vr[h]])
                S.op("dve", lambda e, h=h, vt=vt, blk=blk: e.bn_stats(out=ST[:, (blk % 2) * 12 + h * 6:(blk % 2) * 12 + (h + 1) * 6], in_=vt[:, h, :]),
                     reads=[vr[h]], writes=[f"ST.{blk % 2}.{h}"])
                yield
            q = blk % 2
            S.op("dve", lambda e, q=q: e.bn_aggr(out=MV[:, q * 4:q * 4 + 2], in_=ST[:, q * 12:q * 12 + 12]),
                 reads=[f"ST.{q}.0", f"ST.{q}.1"], writes=[f"MV{q}"])
            S.op("act", lambda e, q=q: e.activation(out=MV[:, q * 4 + 2:q * 4 + 3], in_=MV[:, q * 4 + 1:q * 4 + 2], func=AF.Sqrt, bias=EPS),
                 reads=[f"MV{q}"], writes=[f"MV{q}s"])
            S.op("dve", lambda e, q=q: e.reciprocal(MV[:, q * 4 + 3:q * 4 + 4], MV[:, q * 4 + 2:q * 4 + 3]), reads=[f"MV{q}s"], writes=[f"MV{q}r"])
            S.op("dve", lambda e, q=q, vt=vt: e.tensor_scalar(vt, vt, MV[:, q * 4:q * 4 + 1], MV[:, q * 4 + 3:q * 4 + 4], op0=ALU.subtract, op1=ALU.mult),
                 reads=vr + [f"MV{q}", f"MV{q}r"], writes=vr)
            S.op("dve", lambda e, vt=vt: e.tensor_tensor(out=vt, in0=vt, in1=lng, op=ALU.mult),
                 reads=vr + ["CBC"], writes=vr)
            S.op("dve", lambda e, blk=blk, vt=vt: e.tensor_tensor(out=VNB[:, blk, :].rearrange("p (a t) -> p a t", a=2), in0=vt, in1=lnb, op=ALU.add),
                 reads=vr + ["CBC"], writes=[f"VNB.{blk}"] + zb_over(blk))
        for j in range(NCH):
            if j % 4 == 0:
                sl = acquire()
            b = newbank()
            for k in range(NCH):
                o = ((j % 4) * 8 + k) * 128
                MM(b, SLOTS[sl][:, o:o + 128], H[:, k, :], k == 0, k == NCH - 1, [f"W{sl}", f"{c.p}H.{k}"])
            S.op("act", lambda e, j=j, b=b: e.activation(out=M1[:, j, 0:T], in_=PS[b][:, :], func=AF.Gelu),
                 reads=[f"PS{b}"], writes=[f"M1.{j}"])
            yield
        for g in range(8):
            b = newbank()
            for blk in range(nblk):
                MM(b, VNB[:, blk, g * 128:(g + 1) * 128], WST[:, g, :], True, True, [f"VNB.{blk}", "WST"],
                   cols=(blk * 128, (blk + 1) * 128))
            t = newtmp(3)
            bs = CBC[:, bo["bs"] + g * 128: bo["bs"] + (g + 1) * 128]
            for blk in range(nblk):
                S.op("dve", lambda e, blk=blk, b=b, t=t, bs=bs: e.tensor_tensor(out=TMP[t][:, blk * 128:(blk + 1) * 128],
                                                                         in0=PS[b][:, blk * 128:(blk + 1) * 128], in1=bs, op=ALU.add),
                     reads=[f"PS{b}", "CBC"], writes=[f"TMP{t}"])
            S.op("dve", lambda e, g=g, t=t: e.tensor_tensor(out=ACTB[:, g, :], in0=TMP[t], in1=M1[:, g, 0:T], op=ALU.mult),
                 reads=[f"TMP{t}", f"M1.{g}"], writes=[f"ACTB.{g}"])
            yield
        yield from outproj(c, NCH, lambda k: ACTB[:, k, :], lambda k: f"ACTB.{k}")

    def mixer_b(c, l, seq_start):
        H = c.H
        CB = c.CB
        cbr = f"{c.p}CB"
        if seq_start:
            S.op("dve", lambda e: e.memset(CB[:, :, :], 0.0), writes=[cbr])
        S.op("dve", lambda e: e.tensor_copy(M1[:, :, 0:16], CB[:, :, :]), reads=[cbr], writes=allr("M1"))
        for j in range(NCH):
            if j % 4 == 0:
                sl = acquire()
            b = newbank()
            for k in range(NCH):
                o = ((j % 4) * 8 + k) * 128
                MM(b, SLOTS[sl][:, o:o + 128], H[:, k, :], k == 0, k == NCH - 1, [f"W{sl}", f"{c.p}H.{k}"])
            S.op("act", lambda e, j=j, b=b: e.activation(out=M1[:, j, 16:16 + T], in_=PS[b][:, :], func=AF.Copy),
                 reads=[f"PS{b}"], writes=[f"M1.{j}"])
            yield
        S.op("dve", lambda e: e.tensor_copy(CB[:, :, :], M1[:, :, T:T + 16]), reads=allr("M1"), writes=[cbr])

        def add(out, a, b_, reads, writes):
            S.op("dve", lambda e: e.tensor_tensor(out=out, in0=a, in1=b_, op=ALU.add), reads=reads, writes=writes)
        r = lambda name, lo, hi: [f"{name}.{j}" for j in range(lo, hi)]
        add(Y[:, 0:2, :], M1[:, 0:2, 16:16 + T], M1[:, 0:2, 15:15 + T], r("M1", 0, 2), r("Y", 0, 2))
        add(M2[:, 2:8, 2:16 + T], M1[:, 2:8, 2:16 + T], M1[:, 2:8, 1:15 + T], r("M1", 2, 8), r("M2", 2, 8))
        add(Y[:, 2:4, :], M2[:, 2:4, 16:16 + T], M2[:, 2:4, 14:14 + T], r("M2", 2, 4), r("Y", 2, 4))
        add(M2[:, 0:4, 4:16 + T], M2[:, 4:8, 4:16 + T], M2[:, 4:8, 2:14 + T], r("M2", 4, 8), r("M2", 0, 4))
        add(Y[:, 4:6, :], M2[:, 0:2, 16:16 + T], M2[:, 0:2, 12:12 + T], r("M2", 0, 2), r("Y", 4, 6))
        add(M2[:, 6:8, 8:16 + T], M2[:, 2:4, 8:16 + T], M2[:, 2:4, 4:12 + T], r("M2", 2, 4), r("M2", 6, 8))
        add(Y[:, 6:8, :], M2[:, 6:8, 16:16 + T], M2[:, 6:8, 8:8 + T], r("M2", 6, 8), r("Y", 6, 8))
        for g, w in enumerate((2, 4, 8, 16)):
            for ch in (2 * g, 2 * g + 1):
                S.op("dve", lambda e, ch=ch, w=w: e.scalar_tensor_tensor(out=ACTB[:, ch, :], in0=Y[:, ch, :], scalar=1.0 / w,
                                                                         in1=M1[:, ch, 16:16 + T], op0=ALU.mult, op1=ALU.subtract),
                     reads=[f"Y.{ch}", f"M1.{ch}"], writes=[f"ACTB.{ch}"])
                if seq_start:
                    t = newtmp(3)
                    S.op("dve", lambda e, ch=ch, g=g, t=t: e.tensor_tensor(out=TMP[t][:, 0:16], in0=Y[:, ch, 0:16],
                                                                         in1=CBC[:, bo["invc"] + g * 16: bo["invc"] + (g + 1) * 16], op=ALU.mult),
                         reads=[f"Y.{ch}", "CBC"], writes=[f"TMP{t}"])
                    S.op("dve", lambda e, ch=ch, t=t: e.tensor_tensor(out=ACTB[:, ch, 0:16], in0=TMP[t][:, 0:16], in1=M1[:, ch, 16:32],
                                                                    op=ALU.subtract),
                         reads=[f"TMP{t}", f"M1.{ch}", f"ACTB.{ch}"], writes=[f"ACTB.{ch}"])
        sl = acquire()
        for g in range(4):
            for dj in range(2):
                b = newbank()
                for ci in range(2):
                    o = ((g * 2 + dj) * 2 + ci) * 128
                    MM(b, SLOTS[sl][:, o:o + 128], ACTB[:, 2 * g + ci, :], ci == 0, ci == 1, [f"W{sl}", f"ACTB.{2 * g + ci}"])
                ch = 2 * g + dj
                S.op("act", lambda e, ch=ch, b=b: e.activation(out=ACTB[:, 8 + ch, :], in_=PS[b][:, :], func=AF.Copy, scale=cp("bsc", ch)),
                     reads=[f"PS{b}", "CPP"], writes=[f"ACTB.{8 + ch}"])
            yield
        yield from outproj(c, NCH, lambda k: ACTB[:, 8 + k, :], lambda k: f"ACTB.{8 + k}")

    def mixer_c(c, l, seq_start):
        H = c.H
        CC = c.CC
        ccr = f"{c.p}CC"
        if seq_start:
            S.op("dve", lambda e: e.memset(CC[:, :, :], 0.0), writes=[ccr])
        S.op("dve", lambda e: e.tensor_copy(ZB[:, :, 0:2], CC[:, :, :]), reads=[ccr], writes=allr("ZB") + allr("VNB", 4))

        def conv(j, sl):
            b = newbank()
            for k in range(3):
                o = 3072 + k * 128
                MM(b, SLOTS[sl][:, o:o + 128], ZB[:, j, k:k + T], k == 0, k == 2, [f"W{sl}", f"ZB.{j}"])
            S.op("dve", lambda e: e.tensor_tensor(out=ACTB[:, j, :], in0=PS[b][:, :], in1=M1[:, j, 0:T], op=ALU.mult),
                 reads=[f"PS{b}", f"M1.{j}"], writes=[f"ACTB.{j}"])

        pend = None
        for j in range(NCH):
            sl = acquire(hold=1 if pend is not None else 0)
            bk = []
            for part in range(3):
                b = newbank()
                bk.append(b)
                for k in range(NCH):
                    o = (part * 8 + k) * 128
                    MM(b, SLOTS[sl][:, o:o + 128], H[:, k, :], k == 0, k == NCH - 1, [f"W{sl}", f"{c.p}H.{k}"])
            bb, bc, bx = bk
            t1 = newtmp()
            S.op("act", lambda e, t1=t1, bx=bx: e.activation(out=TMP[t1], in_=PS[bx][:, :], func=AF.Copy),
                 reads=[f"PS{bx}"], writes=[f"TMP{t1}"])
            S.op("dve", lambda e, j=j, t1=t1, bc=bc: e.tensor_tensor(out=ZB[:, j, 2:2 + T], in0=PS[bc][:, :], in1=TMP[t1], op=ALU.mult),
                 reads=[f"PS{bc}", f"TMP{t1}"], writes=[f"ZB.{j}"])
            S.op("act", lambda e, j=j, bb=bb: e.activation(out=M1[:, j, 0:T], in_=PS[bb][:, :], func=AF.Copy),
                 reads=[f"PS{bb}"], writes=[f"M1.{j}"])
            if pend is not None:
                conv(*pend)
            pend = (j, sl)
            yield
        conv(*pend)
        S.op("dve", lambda e: e.tensor_copy(CC[:, :, :], ZB[:, :, T:T + 2]), reads=allr("ZB"), writes=[ccr])
        yield
        yield from outproj(c, NCH, lambda k: ACTB[:, k, :], lambda k: f"ACTB.{k}")

    def mixer_d(c, l, seq_start):
        H = c.H
        CD = c.CD
        cdr = f"{c.p}CD"
        if seq_start:
            S.op("dve", lambda e: e.memset(CD[:, :, :], 0.0), writes=[cdr])
        S.op("dve", lambda e: e.tensor_copy(ZB[:, :, 0:32], CD[:, :, :]), reads=[cdr], writes=allr("ZB") + allr("VNB", 4))
        for j in range(NCH):
            if j % 2 == 0:
                sl = acquire()
            ba = newbank()
            bg = newbank()
            for part, b in ((0, ba), (1, bg)):
                for k in range(NCH):
                    o = (((j % 2) * 2 + part) * 8 + k) * 128
                    MM(b, SLOTS[sl][:, o:o + 128], H[:, k, :], k == 0, k == NCH - 1, [f"W{sl}", f"{c.p}H.{k}"])
            t1 = newtmp()
            S.op("act", lambda e, j=j, t1=t1, bg=bg: e.activation(out=TMP[t1], in_=PS[bg][:, :], func=AF.Sigmoid, bias=cp("db1", 8 + j)),
                 reads=[f"PS{bg}", "CPP"], writes=[f"TMP{t1}"])
            S.op("dve", lambda e, j=j, t1=t1, ba=ba: e.scalar_tensor_tensor(out=ZB[:, j, 32:32 + T], in0=PS[ba][:, :], scalar=cp("db1", j),
                                                                            in1=TMP[t1], op0=ALU.add, op1=ALU.mult),
                 reads=[f"PS{ba}", f"TMP{t1}", "CPP"], writes=[f"ZB.{j}"])
            yield
        S.op("dve", lambda e: e.tensor_copy(CD[:, :, :], ZB[:, :, T:T + 32]), reads=allr("ZB"), writes=[cdr])
        for j in range(NCH):
            sl = acquire()
            b = newbank()
            for k in range(31):
                MM(b, SLOTS[sl][:, k * 128:(k + 1) * 128], ZB[:, j, 2 + k:2 + k + T], k == 0, k == 30, [f"W{sl}", f"ZB.{j}"])
            S.op("act", lambda e, j=j, b=b: e.activation(out=M2[:, j, 0:T], in_=PS[b][:, :], func=AF.Identity, bias=cp("dcb", j)),
                 reads=[f"PS{b}", "CPP"], writes=[f"M2.{j}"])
            S.op("act", lambda e, j=j, b=b: e.activation(out=H[:, j, :], in_=PS[b][:, :], func=AF.Identity, bias=cp("dcb", j)),
                 reads=[f"PS{b}", "CPP"], writes=[f"{c.p}H.{j}"])
            S.op("act", lambda e, j=j, b=b: e.activation(out=ACTB[:, 8 + j, :], in_=PS[b][:, :], func=AF.Square, bias=cp("dcb", j)),
                 reads=[f"PS{b}", "CPP"], writes=[f"ACTB.{8 + j}"])
            yield
        bm = newbank()
        for j in range(NCH):
            MM(bm, ONES[:, :], H[:, j, :], j == 0, j == NCH - 1, ["ONES", f"{c.p}H.{j}"])
        bq = newbank()
        for j in range(NCH):
            MM(bq, ONES[:, :], ACTB[:, 8 + j, :], j == 0, j == NCH - 1, ["ONES", f"ACTB.{8 + j}"])
        tm = 4
        tv = 5
        S.op("act", lambda e: e.activation(out=TMP[tm], in_=PS[bm][:, :], func=AF.Copy), reads=[f"PS{bm}"], writes=[f"TMP{tm}"])
        S.op("dve", lambda e: e.tensor_tensor(out=TMP[tv], in0=TMP[tm], in1=TMP[tm], op=ALU.mult),
             reads=[f"TMP{tm}"], writes=[f"TMP{tv}"])
        S.op("dve", lambda e: e.tensor_tensor(out=TMP[tv], in0=PS[bq][:, :], in1=TMP[tv], op=ALU.subtract),
             reads=[f"PS{bq}", f"TMP{tv}"], writes=[f"TMP{tv}"])
        S.op("act", lambda e: e.activation(out=TMP[tv], in_=TMP[tv], func=AF.Sqrt, bias=EPS), reads=[f"TMP{tv}"], writes=[f"TMP{tv}"])
        S.op("dve", lambda e: e.reciprocal(TMP[tv], TMP[tv]), reads=[f"TMP{tv}"], writes=[f"TMP{tv}"])
        yield
        for j in range(NCH):
            S.op("dve", lambda e, j=j: e.tensor_tensor(out=M2[:, j, 0:T], in0=M2[:, j, 0:T], in1=TMP[tm], op=ALU.subtract),
                 reads=[f"M2.{j}", f"TMP{tm}"], writes=[f"M2.{j}"])
            S.op("dve", lambda e, j=j: e.tensor_tensor(out=M2[:, j, 0:T], in0=M2[:, j, 0:T], in1=TMP[tv], op=ALU.mult),
                 reads=[f"M2.{j}", f"TMP{tv}"], writes=[f"M2.{j}"])
            S.op("act", lambda e, j=j: e.activation(out=ACTB[:, j, :], in_=M2[:, j, 0:T], func=AF.Silu, scale=cp("dlg", j), bias=cp("dlb", j)),
                 reads=[f"M2.{j}", "CPP"], writes=[f"ACTB.{j}"])
        yield from outproj(c, NCH, lambda k: ACTB[:, k, :], lambda k: f"ACTB.{k}", bias_key="db2")

    def ffn(c, l, seq_start):
        H = c.H
        CF = c.CF[l]
        cfr = f"{c.p}CF{l}"
        if seq_start:
            S.op("act", lambda e: e.activation(out=CF[:, :, :], in_=ZERO2[:, 0:88].rearrange("p (a b) -> p a b", b=2), func=AF.Copy),
                 reads=["ZERO2"], writes=[cfr])

        def conv(p, sl, dbase=2048):
            par = p % 2
            src = []
            for gu in range(2):
                chunk = p if gu == 0 else NPAIR + p
                if dve_conv(gu, p):
                    a = AS[gu][par]
                    an = f"AS{gu}{par}"
                    buf = GB[gu][par]
                    bn = f"GB{gu}{par}"
                    for k in (1, 0):
                        S.op("dve", lambda e, k=k, a=a, buf=buf, chunk=chunk: e.scalar_tensor_tensor(
                            out=a[:, :], in0=buf[:, k:k + T], scalar=cp("fcw", (l * 44 + chunk) * 3 + k), in1=a[:, :],
                            op0=ALU.mult, op1=ALU.add), reads=[bn, bn + "x", an, "CPP"], writes=[an])
                    src.append((a[:, :], an))
                else:
                    assert gu == 0
                    b = newbank()
                    for k in range(3):
                        o = dbase + k * 128
                        MM(b, SLOTS[sl][:, o:o + 128], GB[gu][par][:, k:k + T], k == 0, k == 2, [f"W{sl}", f"GB{gu}{par}", f"GB{gu}{par}x"])
                    src.append((PS[b][:, :], f"PS{b}"))
            t = newtmp()
            S.op("act", lambda e: e.activation(out=TMP[t], in_=src[0][0], func=AF.Silu),
                 reads=[src[0][1]], writes=[f"TMP{t}"])
            return (p, t, src[1])

        def gate(p, t, u):
            S.op("dve", lambda e: e.tensor_tensor(out=ACTB[:, p, :], in0=u[0], in1=TMP[t], op=ALU.mult),
                 reads=[f"TMP{t}", u[1]], writes=[f"ACTB.{p}"])

        pend = None
        pend2 = None
        for p in range(NPAIR):
            if pend2 is not None:
                gate(*pend2)
                pend2 = None
            sl = acquire()
            par = p % 2
            banks = []
            for gu in range(2):
                b = newbank()
                banks.append(b)
                for k in range(NCH):
                    o = (gu * 8 + k) * 128
                    MM(b, SLOTS[sl][:, o:o + 128], H[:, k, :], k == 0, k == NCH - 1, [f"W{sl}", f"{c.p}H.{k}"])
            for gu in range(2):
                chunk = p if gu == 0 else NPAIR + p
                buf = GB[gu][par]
                bn = f"GB{gu}{par}"
                b = banks[gu]
                S.op("act", lambda e, buf=buf, chunk=chunk: e.activation(out=buf[:, 0:2], in_=CF[:, chunk, :], func=AF.Copy),
                     reads=[cfr], writes=[bn])
                S.op("act", lambda e, buf=buf, b=b: e.activation(out=buf[:, 2:2 + T], in_=PS[b][:, :], func=AF.Copy),
                     reads=[f"PS{b}"], writes=[bn + "x"])
                S.op("act", lambda e, b=b, chunk=chunk: e.activation(out=CF[:, chunk, :], in_=PS[b][:, T - 2:T], func=AF.Copy),
                     reads=[f"PS{b}"], writes=[cfr])
                if dve_conv(gu, p):
                    a = AS[gu][par]
                    S.op("act", lambda e, a=a, b=b, chunk=chunk: e.activation(out=a[:, :], in_=PS[b][:, :], func=AF.Copy,
                                                                              scale=cp("fcw", (l * 44 + chunk) * 3 + 2)),
                         reads=[f"PS{b}", "CPP"], writes=[f"AS{gu}{par}"])
            if pend is not None:
                pend2 = conv(pend, sl)
            pend = p
            yield
        if pend2 is not None:
            gate(*pend2)
        sl0 = acquire()
        gate(*conv(pend, sl0, NPAIR * 128))
        yield
        yield from outproj(c, NPAIR, lambda k: ACTB[:, k, :], lambda k: f"ACTB.{k}", first_slot=sl0)

    mixers = [mixer_a, mixer_b, mixer_c, mixer_d]
    nsub = len(sub_list)
    stores = {}

    def body(c, ti, si):
        l, k = sub_list[si]
        seq_start = ((c.tok0 + ti * T) % seq) == 0
        def tagged(g, name):
            for _ in g:
                yield
                S.tag = name
        if k == 0:
            S.tag = f"mix{l % 4}"
            return tagged(mixers[l % 4](c, l, seq_start), f"mix{l % 4}")
        S.tag = "ffn"
        return tagged(ffn(c, l, seq_start), "ffn")

    def gpre(si):
        l, k = sub_list[si]
        return (0 if k == 0 else 2) * 4 + l

    def gpost(si):
        l, k = sub_list[si]
        return (1 if k == 0 else 3) * 4 + l

    def load(c, ti):
        n0 = c.tok0 + ti * T
        S.op("pool", lambda e: e.dma_start(out=c.X[:, :, :], in_=xT[:, :, n0:n0 + T]), writes=xr(c), dma=f"xl{c.p}")

    def store(c, ti):
        n0 = c.tok0 + ti * T
        stores[c.p] = S.op("pool", lambda e: e.dma_start(out=oT[:, :, n0:n0 + T], in_=c.X[:, :, :]), reads=xr(c), dma=f"xs{c.p}")

    BODY_STEPS = {("mix", 0): 32, ("mix", 1): 20, ("mix", 2): 17, ("mix", 3): 25, ("ffn", 0): 31}

    def chain(c, ti, si):
        if c is CA:
            l2, k2 = sub_list[si]
        else:
            l2, k2 = sub_list[(si + 1) % nsub]
        steps = BODY_STEPS[("ffn", 0)] if k2 == 1 else BODY_STEPS[("mix", l2 % 4)]
        per = 1 if steps >= 26 else 2
        yield from post_stage(c, gpost(si), per)
        if si == nsub - 1:
            store(c, ti)
            if ti + 1 >= ntiles:
                return
            load(c, ti + 1)
            yield
            yield from pre_sq(c)
            for _ in range(CHAIN_SLACK):
                yield
            yield from pre_norm(c, gpre(0), per)
        else:
            yield
            yield from pre_sq(c)
            for _ in range(CHAIN_SLACK):
                yield
            yield from pre_norm(c, gpre(si + 1), per)

    def first_pre(c):
        pre_stage(c, gpre(0))
        yield

    def interleave(bd, ch):
        for _ in bd:
            if ch is not None:
                try:
                    next(ch)
                except StopIteration:
                    ch = None
        if ch is not None:
            for _ in ch:
                pass

    load(CA, 0)
    load(CBX, 0)
    pre_stage(CA, gpre(0))
    chB = first_pre(CBX)
    for ti in range(ntiles):
        for si in range(nsub):
            interleave(body(CA, ti, si), chB)
            chA = chain(CA, ti, si)
            interleave(body(CBX, ti, si), chA)
            chB = chain(CBX, ti, si)
    for _ in chB:
        pass
    S.final_wait("pool", [stores["A"], stores["B"]])
    S.emit()
    return nc, S


def layout_x(xc):
    n = xc.shape[0]
    return np.ascontiguousarray(xc.T.reshape(NCH, 128, n).transpose(1, 0, 2))


def unlayout_x(o):
    n = o.shape[2]
    return np.ascontiguousarray(o.transpose(1, 0, 2).reshape(D, n).T)


def kernel(**inputs):
    inputs = {k: np.asarray(v) for k, v in inputs.items()}
    x = inputs["x"]
    B, S_, _ = x.shape
    ncores = 8
    per = B // ncores
    ntok = per * S_
    wflat, cpp, cbc, wst = pack_host(inputs)
    nc, _ = build_program(ntok)
    in_maps = []
    for c in range(ncores):
        xc = x[c * per:(c + 1) * per].reshape(ntok, D)
        in_maps.append({"xT": layout_x(xc), "wflat": wflat, "cpp": cpp, "cbc": cbc, "wst": wst})
    res = run_bass_kernel_spmd(nc, in_maps, core_ids=list(range(ncores)))
    out = np.empty((B, S_, D), np.float32)
    for c in range(ncores):
        out[c * per:(c + 1) * per] = unlayout_x(res.results[c]["oT"]).reshape(per, S_, D)
    return out
```
